# Optimizing a Trainium2 kernel written in Bass

```python
import math
import jax, jax.numpy as jnp
from jax import lax
import numpy as np

D_MODEL = 1024
BATCH = 8
SEQ = 2048
DEPTH = 2

HEAD_DIM = 64
CONV_WIDTH = 256
CONV_K = 3
GDN_HEADS = 4
GDN_WIDTH = GDN_HEADS * HEAD_DIM
GDN_CONV_K = 4
GDN_CHUNK = 64
NSA_Q_HEADS = 8
NSA_KV_HEADS = 2
NSA_GROUP = NSA_Q_HEADS // NSA_KV_HEADS
NSA_WIDTH = NSA_Q_HEADS * HEAD_DIM
NSA_KV_WIDTH = 6 * NSA_KV_HEADS * HEAD_DIM
CMP_LEN = 32
CMP_STRIDE = 16
CMP_HIDDEN = 2 * HEAD_DIM
SEL_BLOCK = 64
SEL_TOPN = 8
WINDOW = 512
Q_BLOCK = 128
FORCE_BONUS = 1e3
MIX_WIDTH = CONV_WIDTH + GDN_WIDTH + NSA_WIDTH
IN_SIZES = (CONV_WIDTH, CONV_WIDTH, CONV_WIDTH,
            3 * GDN_WIDTH, GDN_WIDTH, GDN_HEADS, GDN_HEADS,
            NSA_WIDTH, NSA_KV_WIDTH, 3 * NSA_Q_HEADS)
IN_WIDTH = 3 * CONV_WIDTH + 4 * GDN_WIDTH + 2 * GDN_HEADS + NSA_WIDTH + NSA_KV_WIDTH + 3 * NSA_Q_HEADS
D_FF = 2816
DEEPNORM_ALPHA = (2 * DEPTH) ** 0.25
DEEPNORM_BETA = (8 * DEPTH) ** -0.25
LN_EPS = 1e-5
NORM_EPS = 1e-6
NEG = -1e30

kernel_name = "hymba_style_conv_gdn_nsa_macaron_deepnorm"


def layer_norm(x, g, b):
    xf = x.astype(jnp.float32)
    mu = jnp.mean(xf, -1, keepdims=True)
    var = jnp.mean(jnp.square(xf - mu), -1, keepdims=True)
    return ((xf - mu) * lax.rsqrt(var + LN_EPS) * g.astype(jnp.float32) + b.astype(jnp.float32)).astype(x.dtype)


def swiglu(x, w_gate, w_up, w_down):
    return (jax.nn.silu(x @ w_gate) * (x @ w_up)) @ w_down


def causal_dwconv(u, w):
    k = w.shape[-1]
    s = u.shape[1]
    u_pad = jnp.pad(u, ((0, 0), (k - 1, 0), (0, 0)))
    y = u_pad[:, 0:s] * w[:, 0]
    for j in range(1, k):
        y = y + u_pad[:, j:j + s] * w[:, j]
    return y


def l2norm(t):
    return t * lax.rsqrt(jnp.sum(t * t, -1, keepdims=True) + NORM_EPS)


def split_columns(h, sizes):
    out, start = [], 0
    for n in sizes:
        out.append(h[..., start:start + n])
        start += n
    return out


def short_conv_mixer(b_gate, c_gate, h, conv_w):
    return b_gate * causal_dwconv(c_gate * h, conv_w)


def gated_deltanet(qkv, z, a, b, conv_w, a_log, dt_bias, norm_w):
    f32 = jnp.float32
    bsz, s, _ = qkv.shape
    dtype = qkv.dtype
    c = GDN_CHUNK
    nc = s // c
    qkv = jax.nn.silu(causal_dwconv(qkv, conv_w)).astype(f32)
    q, k, v = jnp.split(qkv, 3, axis=-1)
    to_heads = lambda t: t.reshape(bsz, s, GDN_HEADS, HEAD_DIM).transpose(0, 2, 1, 3)
    q = l2norm(to_heads(q)) * HEAD_DIM ** -0.5
    k = l2norm(to_heads(k))
    v = to_heads(v)
    beta = jax.nn.sigmoid(b.astype(f32)).transpose(0, 2, 1)
    g = (-jnp.exp(a_log.astype(f32)) * jax.nn.softplus(a.astype(f32) + dt_bias.astype(f32))).transpose(0, 2, 1)
    chunk = lambda t: t.reshape(bsz, GDN_HEADS, nc, c, *t.shape[3:])
    q, k, v, beta, g = chunk(q), chunk(k), chunk(v), chunk(beta), chunk(g)
    gc = jnp.cumsum(g, axis=-1)
    tril = jnp.tril(jnp.ones((c, c), bool))
    strict = jnp.tril(jnp.ones((c, c), bool), -1)
    diff = gc[..., :, None] - gc[..., None, :]
    decay = jnp.where(tril, jnp.exp(jnp.where(tril, diff, 0.0)), 0.0)
    kb = k * beta[..., None]
    vb = v * beta[..., None]
    lmat = jnp.where(strict, jnp.einsum('bhnid,bhnjd->bhnij', kb, k) * decay, 0.0)
    eye = jnp.eye(c, dtype=f32)
    tinv = lax.linalg.triangular_solve(lmat + eye, jnp.broadcast_to(eye, lmat.shape),
                                       left_side=True, lower=True, unit_diagonal=True)
    u = jnp.einsum('bhnij,bhnjd->bhnid', tinv, vb)
    w = jnp.einsum('bhnij,bhnjd->bhnid', tinv, kb * jnp.exp(gc)[..., None])
    a_qk = jnp.where(tril, jnp.einsum('bhnid,bhnjd->bhnij', q, k) * decay, 0.0)
    g_last = gc[..., -1]
    q_dec = q * jnp.exp(gc)[..., None]
    k_dec = k * jnp.exp(g_last[..., None] - gc)[..., None]

    def step(state, xs):
        q_i, k_i, u_i, w_i, a_i, gl_i = xs
        v_new = u_i - jnp.einsum('bhck,bhkv->bhcv', w_i, state)
        o = jnp.einsum('bhck,bhkv->bhcv', q_i, state) + jnp.einsum('bhij,bhjv->bhiv', a_i, v_new)
        state = state * jnp.exp(gl_i)[..., None, None] + jnp.einsum('bhck,bhcv->bhkv', k_i, v_new)
        return state, o

    xs = tuple(jnp.moveaxis(t, 2, 0) for t in (q_dec, k_dec, u, w, a_qk, g_last))
    state0 = jnp.zeros((bsz, GDN_HEADS, HEAD_DIM, HEAD_DIM), f32)
    _, o = lax.scan(step, state0, xs)
    o = jnp.moveaxis(o, 0, 2).reshape(bsz, GDN_HEADS, s, HEAD_DIM).transpose(0, 2, 1, 3)
    o = o * lax.rsqrt(jnp.mean(o * o, -1, keepdims=True) + NORM_EPS) * norm_w.astype(f32)
    o = o * jax.nn.silu(z.astype(f32).reshape(bsz, s, GDN_HEADS, HEAD_DIM))
    return o.reshape(bsz, s, GDN_WIDTH).astype(dtype)


def nsa_attention(q, kv, gates, pe_k, pe_v, ck_w1, ck_w2, cv_w1, cv_w2):
    f32 = jnp.float32
    bsz, s, _ = q.shape
    dtype = q.dtype
    hk, grp, d = NSA_KV_HEADS, NSA_GROUP, HEAD_DIM
    scale = HEAD_DIM ** -0.5
    q = q.reshape(bsz, s, hk, grp, d).transpose(0, 2, 3, 1, 4)
    kv = kv.reshape(bsz, s, 6, hk, d)
    k_cmp, v_cmp, k_slc, v_slc, k_win, v_win = (kv[:, :, i] for i in range(6))
    t_pos = jnp.arange(s)

    n_cmp = (s - CMP_LEN) // CMP_STRIDE + 1
    starts = jnp.arange(n_cmp) * CMP_STRIDE
    blk_idx = starts[:, None] + jnp.arange(CMP_LEN)[None, :]

    def compress(t, pe, w1, w2):
        blocks = t[:, blk_idx] + pe[None, None, :, None, :]
        flat = blocks.transpose(0, 1, 3, 2, 4).reshape(bsz, n_cmp, hk, CMP_LEN * d)
        return jax.nn.silu(flat @ w1) @ w2

    kc = compress(k_cmp, pe_k, ck_w1, ck_w2)
    vc = compress(v_cmp, pe_v, cv_w1, cv_w2)
    s_cmp = jnp.einsum('bhgtd,bnhd->bhgtn', q, kc).astype(f32) * scale
    cmp_ok = (starts + CMP_LEN - 1)[None, :] <= t_pos[:, None]
    p_cmp = jax.nn.softmax(jnp.where(cmp_ok, s_cmp, NEG), axis=-1)
    p_cmp = jnp.where(cmp_ok.any(-1)[:, None], p_cmp, 0.0)
    o_cmp = jnp.einsum('bhgtn,bnhd->bhgtd', p_cmp.astype(dtype), vc)

    n_sel = s // SEL_BLOCK
    n_top = min(SEL_TOPN, n_sel)
    cmp_tok = (t_pos[None, :] >= starts[:, None]) & (t_pos[None, :] < starts[:, None] + CMP_LEN)
    sel_tok = (t_pos[None, :] // SEL_BLOCK) == jnp.arange(n_sel)[:, None]
    overlap = (cmp_tok.astype(f32) @ sel_tok.astype(f32).T) / CMP_LEN
    imp = jnp.einsum('bhgtn,nj->bhtj', p_cmp, overlap)
    q_blk = t_pos // SEL_BLOCK
    j = jnp.arange(n_sel)
    forced = (j[None, :] == 0) | (j[None, :] == q_blk[:, None]) | (j[None, :] == q_blk[:, None] - 1)
    imp = jnp.where(forced, imp + FORCE_BONUS, imp)
    imp = jnp.where(j[None, :] <= q_blk[:, None], imp, NEG)
    _, sel_idx = lax.top_k(imp, n_top)

    kb_slc = k_slc.reshape(bsz, n_sel, SEL_BLOCK, hk, d).transpose(0, 3, 1, 2, 4)
    vb_slc = v_slc.reshape(bsz, n_sel, SEL_BLOCK, hk, d).transpose(0, 3, 1, 2, 4)
    kw = jnp.pad(k_win.transpose(0, 2, 1, 3), ((0, 0), (0, 0), (WINDOW, 0), (0, 0)))
    vw = jnp.pad(v_win.transpose(0, 2, 1, 3), ((0, 0), (0, 0), (WINDOW, 0), (0, 0)))
    n_qb = s // Q_BLOCK
    b_ix = jnp.arange(bsz)[:, None, None, None]
    h_ix = jnp.arange(hk)[None, :, None, None]

    def block_attn(args):
        qb, idx, i = args
        tq = i * Q_BLOCK + jnp.arange(Q_BLOCK)
        kg = kb_slc[b_ix, h_ix, idx].reshape(bsz, hk, Q_BLOCK, n_top * SEL_BLOCK, d)
        vg = vb_slc[b_ix, h_ix, idx].reshape(bsz, hk, Q_BLOCK, n_top * SEL_BLOCK, d)
        kpos = (idx[..., None] * SEL_BLOCK + jnp.arange(SEL_BLOCK)).reshape(bsz, hk, Q_BLOCK, n_top * SEL_BLOCK)
        ok = kpos <= tq[None, None, :, None]
        sc = jnp.einsum('bhgqd,bhqkd->bhgqk', qb, kg).astype(f32) * scale
        ps = jax.nn.softmax(jnp.where(ok[:, :, None], sc, NEG), axis=-1)
        o_s = jnp.einsum('bhgqk,bhqkd->bhgqd', ps.astype(dtype), vg)
        kwin = lax.dynamic_slice_in_dim(kw, i * Q_BLOCK, WINDOW + Q_BLOCK, axis=2)
        vwin = lax.dynamic_slice_in_dim(vw, i * Q_BLOCK, WINDOW + Q_BLOCK, axis=2)
        wpos = i * Q_BLOCK - WINDOW + jnp.arange(WINDOW + Q_BLOCK)
        dist = tq[:, None] - wpos[None, :]
        wok = (dist >= 0) & (dist < WINDOW) & (wpos[None, :] >= 0)
        sw = jnp.einsum('bhgqd,bhkd->bhgqk', qb, kwin).astype(f32) * scale
        pw = jax.nn.softmax(jnp.where(wok, sw, NEG), axis=-1)
        o_w = jnp.einsum('bhgqk,bhkd->bhgqd', pw.astype(dtype), vwin)
        return o_s, o_w

    q_blocks = q.reshape(bsz, hk, grp, n_qb, Q_BLOCK, d).transpose(3, 0, 1, 2, 4, 5)
    idx_blocks = sel_idx.reshape(bsz, hk, n_qb, Q_BLOCK, n_top).transpose(2, 0, 1, 3, 4)
    o_slc, o_win = lax.map(block_attn, (q_blocks, idx_blocks, jnp.arange(n_qb)))
    unblock = lambda o: o.transpose(1, 2, 3, 0, 4, 5).reshape(bsz, hk, grp, s, d)
    to_tokens = lambda o: o.transpose(0, 3, 1, 2, 4).reshape(bsz, s, NSA_Q_HEADS, d)
    gt = jax.nn.sigmoid(gates.astype(f32)).reshape(bsz, s, NSA_Q_HEADS, 3).astype(dtype)
    o = (gt[..., 0:1] * to_tokens(o_cmp) + gt[..., 1:2] * to_tokens(unblock(o_slc))
         + gt[..., 2:3] * to_tokens(unblock(o_win)))
    return o.reshape(bsz, s, NSA_WIDTH)


def hybrid_mixer(x, w_in, conv_w, gdn_conv_w, gdn_a_log, gdn_dt_bias, gdn_norm_w,
                 cmp_pe_k, cmp_pe_v, cmp_k_w1, cmp_k_w2, cmp_v_w1, cmp_v_w2, w_out):
    h = x @ w_in
    cb, cc, ch, gqkv, gz, ga, gb, nq, nkv, ngate = split_columns(h, IN_SIZES)
    y_a = short_conv_mixer(cb, cc, ch, conv_w)
    y_b = gated_deltanet(gqkv, gz, ga, gb, gdn_conv_w, gdn_a_log, gdn_dt_bias, gdn_norm_w)
    y_c = nsa_attention(nq, nkv, ngate, cmp_pe_k, cmp_pe_v, cmp_k_w1, cmp_k_w2, cmp_v_w1, cmp_v_w2)
    return jnp.concatenate([y_a, y_b, y_c], axis=-1) @ w_out


def setup_inputs(seed: int = 0) -> dict:
    key = jax.random.key(seed)
    keys = jax.random.split(key, 26)
    f32 = jnp.float32
    L = DEPTH

    def nrm(i, shape, scale):
        return jax.random.normal(keys[i], shape, f32) * scale

    dt = jnp.exp(jax.random.uniform(keys[10], (L, GDN_HEADS), f32, math.log(1e-3), math.log(1e-1)))
    return {
        "x": nrm(0, (BATCH, SEQ, D_MODEL), 1.0),
        "ffn1_w_gate": nrm(1, (L, D_MODEL, D_FF), D_MODEL ** -0.5),
        "ffn1_w_up": nrm(2, (L, D_MODEL, D_FF), D_MODEL ** -0.5),
        "ffn1_w_down": nrm(3, (L, D_FF, D_MODEL), D_FF ** -0.5 * DEEPNORM_BETA),
        "ln1_g": 1.0 + nrm(4, (L, D_MODEL), 0.02),
        "ln1_b": nrm(5, (L, D_MODEL), 0.02),
        "w_in": nrm(6, (L, D_MODEL, IN_WIDTH), D_MODEL ** -0.5),
        "conv_w": nrm(7, (L, CONV_WIDTH, CONV_K), CONV_K ** -0.5),
        "gdn_conv_w": nrm(8, (L, 3 * GDN_WIDTH, GDN_CONV_K), GDN_CONV_K ** -0.5),
        "gdn_a_log": jnp.log(jax.random.uniform(keys[9], (L, GDN_HEADS), f32, 1.0, 16.0)),
        "gdn_dt_bias": dt + jnp.log(-jnp.expm1(-dt)),
        "gdn_norm_w": 1.0 + nrm(11, (L, HEAD_DIM), 0.02),
        "cmp_pe_k": nrm(12, (L, CMP_LEN, HEAD_DIM), 0.1),
        "cmp_pe_v": nrm(13, (L, CMP_LEN, HEAD_DIM), 0.1),
        "cmp_k_w1": nrm(14, (L, CMP_LEN * HEAD_DIM, CMP_HIDDEN), (CMP_LEN * HEAD_DIM) ** -0.5),
        "cmp_k_w2": nrm(15, (L, CMP_HIDDEN, HEAD_DIM), CMP_HIDDEN ** -0.5),
        "cmp_v_w1": nrm(16, (L, CMP_LEN * HEAD_DIM, CMP_HIDDEN), (CMP_LEN * HEAD_DIM) ** -0.5),
        "cmp_v_w2": nrm(17, (L, CMP_HIDDEN, HEAD_DIM), CMP_HIDDEN ** -0.5),
        "w_out": nrm(18, (L, MIX_WIDTH, D_MODEL), MIX_WIDTH ** -0.5 * DEEPNORM_BETA),
        "ln2_g": 1.0 + nrm(19, (L, D_MODEL), 0.02),
        "ln2_b": nrm(20, (L, D_MODEL), 0.02),
        "ffn2_w_gate": nrm(21, (L, D_MODEL, D_FF), D_MODEL ** -0.5),
        "ffn2_w_up": nrm(22, (L, D_MODEL, D_FF), D_MODEL ** -0.5),
        "ffn2_w_down": nrm(23, (L, D_FF, D_MODEL), D_FF ** -0.5 * DEEPNORM_BETA),
        "ln3_g": 1.0 + nrm(24, (L, D_MODEL), 0.02),
        "ln3_b": nrm(25, (L, D_MODEL), 0.02),
    }


def reference(x, ffn1_w_gate, ffn1_w_up, ffn1_w_down, ln1_g, ln1_b, w_in, conv_w, gdn_conv_w,
              gdn_a_log, gdn_dt_bias, gdn_norm_w, cmp_pe_k, cmp_pe_v, cmp_k_w1, cmp_k_w2,
              cmp_v_w1, cmp_v_w2, w_out, ln2_g, ln2_b, ffn2_w_gate, ffn2_w_up, ffn2_w_down,
              ln3_g, ln3_b):
    for l in range(DEPTH):
        x = layer_norm(DEEPNORM_ALPHA * x + 0.5 * swiglu(x, ffn1_w_gate[l], ffn1_w_up[l], ffn1_w_down[l]),
                       ln1_g[l], ln1_b[l])
        x = layer_norm(DEEPNORM_ALPHA * x + hybrid_mixer(x, w_in[l], conv_w[l], gdn_conv_w[l], gdn_a_log[l],
                                                         gdn_dt_bias[l], gdn_norm_w[l], cmp_pe_k[l], cmp_pe_v[l],
                                                         cmp_k_w1[l], cmp_k_w2[l], cmp_v_w1[l], cmp_v_w2[l],
                                                         w_out[l]),
                       ln2_g[l], ln2_b[l])
        x = layer_norm(DEEPNORM_ALPHA * x + 0.5 * swiglu(x, ffn2_w_gate[l], ffn2_w_up[l], ffn2_w_down[l]),
                       ln3_g[l], ln3_b[l])
    return x
```

```python
import numpy as np
from contextlib import ExitStack
import concourse.bass as bass
import concourse.mybir as mybir
from concourse.bass_utils import run_bass_kernel_spmd

F32 = mybir.dt.float32
BF16 = mybir.dt.bfloat16
AF = mybir.ActivationFunctionType
ALU = mybir.AluOpType
AX = mybir.AxisListType

D_MODEL = 1024
SEQ = 2048
DEPTH = 2
D_FF = 2816
IN_WIDTH = 3104
NT = SEQ // 128
KC = D_MODEL // 128
FC = D_FF // 128
ALPHA = float((2 * DEPTH) ** 0.25)
LN_EPS = 1e-5
NORM_EPS = 1e-6
NEG = -1e30

O_CB, O_CC, O_CH = 0, 256, 512
O_GQKV = 768
O_GZ = O_GQKV + 768
O_GA = O_GZ + 256
O_GB = O_GA + 4
O_NQ = O_GB + 4
O_NKV = O_NQ + 512
O_NG = O_NKV + 768


class Dep:
    __slots__ = ("w", "r")

    def __init__(self):
        self.w = None
        self.r = {}


class Sched:
    ENG = ("pe", "dve", "act", "pool", "sp")

    def __init__(self, nc, es, n_dma_sems=24):
        self.nc = nc
        self.q = {e: [] for e in self.ENG}
        self.cnt = {e: 0 for e in self.ENG}
        self.waited = {e: {} for e in self.ENG}
        self.sems = {}
        for e in ("pe", "dve", "act", "pool"):
            self.sems[e] = es.enter_context(nc.semaphore("s_" + e))
        self.dma_keys = []
        self.dma_pool = {"sp": [], "pool": []}
        for qn, n in (("sp", n_dma_sems), ("pool", 12)):
            for i in range(n):
                k = "d%s%d" % (qn, i)
                self.sems[k] = es.enter_context(nc.semaphore("s_" + k))
                self.dma_keys.append(k)
                self.dma_pool[qn].append(k)
        self.dma_val = {k: 0 for k in self.dma_keys}
        self.dma_i = {"sp": 0, "pool": 0}
        self.out_tokens = []

    def _emit_waits(self, e, waits):
        for k, v in waits.items():
            if self.waited[e].get(k, 0) < v:
                self.waited[e][k] = v
                sem = self.sems[k]
                self.q[e].append(lambda eng, sem=sem, v=v: eng.wait_ge(sem, v))

    @staticmethod
    def _add(waits, tok):
        k, v = tok
        if waits.get(k, 0) < v:
            waits[k] = v

    def _collect(self, e, reads, writes):
        waits = {}
        for d in reads:
            if d.w is not None:
                self._add(waits, d.w)
        for d in writes:
            if d.w is not None and (d.w[0] != e or e != "pe"):
                self._add(waits, d.w)
            for k, v in d.r.items():
                if k != e or e != "pe":
                    self._add(waits, (k, v))
        return waits

    def op(self, e, fn, reads=(), writes=(), inc=True):
        self._emit_waits(e, self._collect(e, reads, writes))
        if inc:
            self.cnt[e] += 1
            idx = self.cnt[e]
            sem = self.sems[e]
            self.q[e].append(lambda eng, fn=fn, sem=sem: fn(eng).then_inc(sem, 1))
        else:
            idx = self.cnt[e] + 1
            self.q[e].append(lambda eng, fn=fn: fn(eng))
        for d in reads:
            if d.r.get(e, 0) < idx:
                d.r[e] = idx
        for d in writes:
            d.w = (e, idx)
            d.r = {}
        return (e, idx)

    def pe(self, fn, reads=(), writes=(), inc=True):
        return self.op("pe", fn, reads, writes, inc)

    def dve(self, fn, reads=(), writes=()):
        return self.op("dve", fn, reads, writes)

    def act(self, fn, reads=(), writes=()):
        return self.op("act", fn, reads, writes)

    def pool(self, fn, reads=(), writes=()):
        return self.op("pool", fn, reads, writes)

    def dma(self, qe, out, in_, reads=(), writes=(), is_output=False, **kw):
        waits = self._collect("__dma__", reads, writes)
        pl = self.dma_pool[qe]
        k = pl[self.dma_i[qe] % len(pl)]
        self.dma_i[qe] += 1
        if self.dma_val[k] > 0:
            self._add(waits, (k, self.dma_val[k]))
        self._emit_waits(qe, waits)
        self.dma_val[k] += 16
        v = self.dma_val[k]
        sem = self.sems[k]
        self.q[qe].append(lambda eng, out=out, in_=in_, sem=sem, kw=kw:
                          eng.dma_start(out=out, in_=in_, **kw).then_inc(sem, 16))
        for d in reads:
            if d.r.get(k, 0) < v:
                d.r[k] = v
        for d in writes:
            d.w = (k, v)
            d.r = {}
        if is_output:
            self.out_tokens.append((k, v))
        return (k, v)

    def barrier(self):
        allw = {}
        for e in ("pe", "dve", "act", "pool"):
            if self.cnt[e] > 0:
                allw[e] = self.cnt[e]
        for k in self.dma_keys:
            if self.dma_val[k] > 0:
                allw[k] = self.dma_val[k]
        for e in self.ENG:
            self._emit_waits(e, dict(allw))

    def finish(self):
        waits = {}
        for t in self.out_tokens:
            self._add(waits, t)
        self._emit_waits("sp", waits)

    def run(self, block):
        nc = self.nc
        q = self.q

        @block.tensor
        def _(eng):
            for f in q["pe"]:
                f(eng)

        @block.vector
        def _(eng):
            for f in q["dve"]:
                f(eng)

        @block.scalar
        def _(eng):
            for f in q["act"]:
                f(eng)

        @block.gpsimd
        def _(eng):
            for f in q["pool"]:
                f(eng)

        @block.sync
        def _(eng):
            for f in q["sp"]:
                f(eng)


class Ctx:
    pass


class Arena:
    def __init__(self, t, nelem):
        self.t = t
        self.n = nelem
        self.off = 0
        self.log = []

    def reset(self):
        self.off = 0
        self.log = []

    def alloc(self, free_shape, dt, parts=128):
        n = 1
        for v in free_shape:
            n *= v
        ne = n * 2 if dt == F32 else n
        self.off = (self.off + 1) // 2 * 2
        assert self.off + ne <= self.n, ("arena overflow", self.off, ne, self.n)
        v = self.t[0:parts, self.off:self.off + ne]
        self.log.append((self.off, ne))
        self.off += ne
        if dt == F32:
            v = v.bitcast(F32)
        if len(free_shape) == 2:
            v = v.rearrange("p (a b) -> p a b", a=free_shape[0])
        elif len(free_shape) == 3:
            v = v.rearrange("p (a b c) -> p a b c", a=free_shape[0], b=free_shape[1])
        elif len(free_shape) == 4:
            v = v.rearrange("p (a b c d) -> p a b c d", a=free_shape[0], b=free_shape[1], c=free_shape[2])
        return v


def build(stage="full", dbg=(), skip_ffn=False):
    nc = bass.Bass("TRN2", target_bir_lowering=False)
    es = ExitStack()
    c = Ctx()
    c.nc = nc
    c.es = es
    c.dbg = {}
    c.no_arena_dump = ("noarena" in dbg)

    def dram_in(name, shape, dt=F32):
        return nc.dram_tensor(name, list(shape), dt, kind="ExternalInput").ap()

    L = DEPTH
    I = c.I = {}
    I["x"] = dram_in("x", (SEQ, D_MODEL))
    for nm, shp in [
        ("ffn1_w_gate", (L, D_MODEL, D_FF)), ("ffn1_w_up", (L, D_MODEL, D_FF)),
        ("ffn1_w_down", (L, D_FF, D_MODEL)), ("ln1_g", (L, D_MODEL)), ("ln1_b", (L, D_MODEL)),
        ("w_in", (L, D_MODEL, IN_WIDTH)), ("conv_w", (L, 256, 3)), ("gdn_conv_w", (L, 768, 4)),
        ("gdn_a_log", (L, 4)), ("gdn_dt_bias", (L, 4)), ("gdn_norm_w", (L, 64)),
        ("cmp_pe_k", (L, 32, 64)), ("cmp_pe_v", (L, 32, 64)),
        ("cmp_k_w1", (L, 2048, 128)), ("cmp_k_w2", (L, 128, 64)),
        ("cmp_v_w1", (L, 2048, 128)), ("cmp_v_w2", (L, 128, 64)),
        ("w_out", (L, D_MODEL, D_MODEL)), ("ln2_g", (L, D_MODEL)), ("ln2_b", (L, D_MODEL)),
        ("ffn2_w_gate", (L, D_MODEL, D_FF)), ("ffn2_w_up", (L, D_MODEL, D_FF)),
        ("ffn2_w_down", (L, D_FF, D_MODEL)), ("ln3_g", (L, D_MODEL)), ("ln3_b", (L, D_MODEL)),
    ]:
        I[nm] = dram_in(nm, shp)
    I["ident"] = dram_in("c_ident", (128, 128))
    for nm, shp in [("bu", (128, 128)), ("he", (128, 128)), ("ho", (128, 128)), ("m1", (128, 128)), ("m2", (128, 128)),
                    ("cm", (127, NT * 128)), ("fb", (128, NT * 32)), ("ex", (32, NT * 128)), ("caus", (128, 128)),
                    ("win", (128, 128)), ("ov", (127, 32)), ("rk", (128, 6 * 256))]:
        I[nm] = dram_in("c_" + nm, shp)
    out = c.out = nc.dram_tensor("out", [SEQ, D_MODEL], F32, kind="ExternalOutput").ap()

    S = c.S = Sched(nc, es)

    def sb(name, shape, dt=F32):
        return es.enter_context(nc.sbuf_tensor(name, list(shape), dt))

    def ps(name, shape, dt=F32):
        return es.enter_context(nc.psum_tensor(name, list(shape), dt))

    c.sb, c.ps = sb, ps

    X = c.X = sb("X", [128, NT, D_MODEL], F32)
    XT = c.XT = sb("XT", [128, KC, SEQ], BF16)
    c.dX = [Dep() for _ in range(NT)]
    c.dXT = [Dep() for _ in range(NT)]
    ident_f = c.ident_f = sb("ident_f", [128, 128], F32)
    ident_b = c.ident_b = sb("ident_b", [128, 128], BF16)
    c.d_ident = Dep()
    S.dma("sp", ident_f[:], I["ident"][:, :], writes=[c.d_ident])
    S.dve(lambda e: e.tensor_copy(out=ident_b[:], in_=ident_f[:]), reads=[c.d_ident], writes=[c.d_ident])

    c.stage = stage
    skind = "ExternalOutput" if stage[0] in "PCGN" else "Internal"
    c.HT = nc.dram_tensor("scr_ht", [14 * 128, SEQ], F32, kind=skind).ap()
    c.HB = nc.dram_tensor("scr_hb", [6 * 128, SEQ], BF16, kind=skind).ap()
    c.HK = nc.dram_tensor("scr_hk", [SEQ, 544], F32, kind=skind).ap()
    c.dHT = [[Dep() for _ in range(4)] for _ in range(14)]
    c.dHB = [[Dep() for _ in range(4)] for _ in range(6)]
    c.dHK = [Dep() for _ in range(NT)]
    c.WBs = [nc.dram_tensor("scr_wb%d" % i, [FC // 2, 128, 2 * KC * 256], BF16, kind="Internal").ap() for i in range(2)]
    c.dWBs = [[Dep() for _ in range(FC // 2)] for i in range(2)]
    c.WDs = [nc.dram_tensor("scr_wd%d" % i, [128, FC * D_MODEL], BF16, kind="Internal").ap() for i in range(2)]
    c.dWDs = [[Dep() for _ in range(4)] for i in range(2)]
    c.ffn_i = 0
    c.preconv = set()
    c.QC = nc.dram_tensor("scr_qc", [6 * 128, SEQ], F32, kind="Internal").ap()
    c.dQC = [Dep() for _ in range(6)]
    c.ab_sb = sb("ab_sb", [128, NT, 8], F32)
    c.gt_sb = sb("gt_sb", [128, NT, 24], F32)
    c.d_ab, c.d_gt = Dep(), Dep()

    c.bank = [ps("bank%d" % i, [128, 512], F32) for i in range(8)]
    c.dbank = [Dep() for _ in range(8)]

    xv = I["x"].rearrange("(t p) d -> p t d", p=128)
    for q4 in range(4):
        S.dma("sp", X[:, q4 * 4:(q4 + 1) * 4, :], xv[:, q4 * 4:(q4 + 1) * 4, :],
              writes=[c.dX[t] for t in range(q4 * 4, q4 * 4 + 4)])
    ARENA_N = 53248
    c.arena = Arena(sb("arena", [128, ARENA_N], BF16), ARENA_N)
    xb_tmp = c.xb_tmp = sb("xb_tmp", [128, D_MODEL], BF16)
    c.d_xb = Dep()
    for tt in range(NT):
        emit_transposes(c, tt, src_f32=True)

    for l in range(DEPTH):
        if not skip_ffn:
            ffn(c, l, 1)
        if stage == "ffn1" and l == 0:
            break
        mixer(c, l)
        if (stage[0] in "PCGN" or stage == "mix") and l == 0:
            break
        ffn(c, l, 2)
        if stage == "layer0":
            break

    if stage[0] in "CGN":
        dbg_xt = nc.dram_tensor("dbg_xt", [128, KC, SEQ], BF16, kind="ExternalOutput").ap()
        S.dma("sp", dbg_xt, XT[:], reads=list(c.dXT), is_output=True)
    ov = out.rearrange("(t p) d -> p t d", p=128)
    for q4 in range(4):
        S.dma("sp", ov[:, q4 * 4:(q4 + 1) * 4, :], X[:, q4 * 4:(q4 + 1) * 4, :],
              reads=[c.dX[t] for t in range(q4 * 4, q4 * 4 + 4)], is_output=True)
    S.finish()
    with nc.Block() as block:
        S.run(block)
    es.close()
    return nc


def emit_transposes(c, tt, src_f32=True):
    S = c.S
    X, XT = c.X, c.XT
    xb = c.xb_tmp
    S.act(lambda e: e.activation(out=xb[:], in_=X[:, tt, :], func=AF.Copy),
          reads=[c.dX[tt]], writes=[c.d_xb])
    bk = 7
    pt = c.bank[bk][:].bitcast(BF16)
    for k in range(KC):
        S.pe(lambda e, k=k: e.transpose(out=pt[:, k * 128:(k + 1) * 128], in_=xb[:, k * 128:(k + 1) * 128],
                                        identity=c.ident_b[:]),
             reads=[c.d_xb, c.d_ident], writes=[c.dbank[bk]], inc=(k == KC - 1))
    S.dve(lambda e: e.tensor_copy(out=XT[:, :, tt * 128:(tt + 1) * 128],
                                  in_=pt.rearrange("p (k t) -> p k t", k=KC)),
          reads=[c.dbank[bk]], writes=[c.dXT[tt]])


def ffn(c, l, which):
    S, nc, I = c.S, c.nc, c.I
    X, XT = c.X, c.XT
    pre = "ffn%d_" % which
    wg_d, wu_d, wd_d = I[pre + "w_gate"], I[pre + "w_up"], I[pre + "w_down"]
    lg_d, lb_d = (I["ln1_g"], I["ln1_b"]) if which == 1 else (I["ln3_g"], I["ln3_b"])
    S.barrier()
    A = c.arena
    A.reset()
    fb = c.ffn_bufs = Ctx()
    ln_bufs(c, fb)
    fb.wd = A.alloc((FC, D_MODEL), BF16)
    fb.d_wd = [Dep() for _ in range(4)]
    fb.hT = A.alloc((FC, 512), BF16)
    fb.d_hT = [Dep() for _ in range(FC)]
    fb.NS = 2
    fb.wgu = [A.alloc((2, KC, 256), BF16) for i in range(fb.NS)]
    fb.d_wgu = [Dep() for _ in range(fb.NS)]
    fb.sg = [A.alloc((512,), F32) for i in range(2)]
    fb.d_sg = [Dep() for _ in range(2)]
    fb.slot_i = 0
    fb.ybank_i = 0
    S.dma("sp", fb.G[:], lg_d[l:l + 1, :].broadcast_to([128, D_MODEL]), writes=[fb.d_GB])
    S.dma("sp", fb.B[:], lb_d[l:l + 1, :].broadcast_to([128, D_MODEL]), writes=[fb.d_GB])
    wgv = wg_d[l].rearrange("(k p) f -> p k f", p=128)
    wuv = wu_d[l].rearrange("(k p) f -> p k f", p=128)
    wdv = wd_d[l].rearrange("(f p) m -> p f m", p=128)
    wd_loaded = False
    fidx = l * 2 + (which - 1)
    c.WB, c.dWB = c.WBs[fidx % 2], c.dWBs[fidx % 2]
    pre = fidx in c.preconv
    if which == 2 and l + 1 < DEPTH:
        preconvert_ffn(c, l + 1, 1)
    for tb in range(4):
        tsl = slice(tb * 512, (tb + 1) * 512)
        for fg in range(FC // 2):
            s = fb.slot_i % fb.NS
            fb.slot_i += 1
            wt = fb.wgu[s]
            if tb == 0 and not pre:
                S.dma("pool", wt[:, 0, :, :], wgv[:, :, fg * 256:(fg + 1) * 256], writes=[fb.d_wgu[s]])
                S.dma("pool", wt[:, 1, :, :], wuv[:, :, fg * 256:(fg + 1) * 256], writes=[fb.d_wgu[s]])
                S.dma("sp", c.WB[fg], wt[:].rearrange("p a k f -> p (a k f)"), reads=[fb.d_wgu[s]], writes=[c.dWB[fg]])
            else:
                S.dma("sp", wt[:].rearrange("p a k f -> p (a k f)"), c.WB[fg], reads=[c.dWB[fg]], writes=[fb.d_wgu[s]])
            if tb == 0 and not wd_loaded and fg == 1:
                for qd in range(4):
                    fsl = slice(qd * 6, min(FC, (qd + 1) * 6))
                    if pre:
                        S.dma("sp", fb.wd[:, fsl, :], c.WDs[fidx % 2].rearrange("p (f m) -> p f m", f=FC)[:, fsl, :],
                              reads=[c.dWDs[fidx % 2][qd]], writes=[fb.d_wd[qd]])
                    else:
                        S.dma("pool", fb.wd[:, fsl, :], wdv[:, fsl, :], writes=[fb.d_wd[qd]])
                wd_loaded = True
            for fc2 in range(2):
                f = fg * 2 + fc2
                gb, ub = (0, 1) if f % 2 == 0 else (2, 3)
                for (bk, wi) in ((gb, 0), (ub, 1)):
                    for k in range(KC):
                        S.pe(lambda e, bk=bk, wi=wi, k=k, wt=wt, fc2=fc2, tsl=tsl:
                             e.matmul(c.bank[bk][:], lhsT=wt[:, wi, k, fc2 * 128:(fc2 + 1) * 128],
                                      rhs=XT[:, k, tsl], start=(k == 0), stop=(k == KC - 1)),
                             reads=[fb.d_wgu[s]] + [c.dXT[tb * 4 + i] for i in range(4)],
                             writes=[c.dbank[bk]], inc=(k == KC - 1))
                sg = fb.sg[f % 2]
                S.act(lambda e, sg=sg, gb=gb: e.activation(out=sg[:], in_=c.bank[gb][:], func=AF.Silu),
                      reads=[c.dbank[gb]], writes=[fb.d_sg[f % 2]])
                S.dve(lambda e, sg=sg, ub=ub, f=f: e.scalar_tensor_tensor(
                    out=fb.hT[:, f, :], in0=c.bank[ub][:], scalar=0.5, in1=sg[:], op0=ALU.mult, op1=ALU.mult),
                    reads=[c.dbank[ub], fb.d_sg[f % 2]], writes=[fb.d_hT[f]])
        for ti in range(4):
            tt = tb * 4 + ti
            for half in range(2):
                bk = 4 + (fb.ybank_i % 3)
                fb.ybank_i += 1
                for f in range(FC):
                    S.pe(lambda e, bk=bk, f=f, ti=ti, half=half:
                         e.matmul(c.bank[bk][:], lhsT=fb.hT[:, f, ti * 128:(ti + 1) * 128],
                                  rhs=fb.wd[:, f, half * 512:(half + 1) * 512], start=(f == 0), stop=(f == FC - 1)),
                         reads=[fb.d_hT[f], fb.d_wd[min(3, f // 6)]], writes=[c.dbank[bk]], inc=(f == FC - 1))
                residual_half(c, tt, half, bk)
            flush_transposes(c)
            layer_norm_tile(c, tt)
    flush_transposes(c)


def preconvert_ffn(c, l, which):
    S, I = c.S, c.I
    pre = "ffn%d_" % which
    fidx = l * 2 + (which - 1)
    WB, dWB = c.WBs[fidx % 2], c.dWBs[fidx % 2]
    wgv = I[pre + "w_gate"][l].rearrange("(k p) f -> p k f", p=128)
    wuv = I[pre + "w_up"][l].rearrange("(k p) f -> p k f", p=128)
    wdv = I[pre + "w_down"][l].rearrange("(f p) m -> p f m", p=128)
    for fg in range(FC // 2):
        dst = WB[fg].rearrange("p (a k f) -> p a k f", a=2, k=KC)
        S.dma("pool", dst[:, 0, :, :], wgv[:, :, fg * 256:(fg + 1) * 256], writes=[dWB[fg]])
        S.dma("pool", dst[:, 1, :, :], wuv[:, :, fg * 256:(fg + 1) * 256], writes=[dWB[fg]])
    wdd = c.WDs[fidx % 2].rearrange("p (f m) -> p f m", f=FC)
    for qd in range(4):
        fsl = slice(qd * 6, min(FC, (qd + 1) * 6))
        S.dma("pool", wdd[:, fsl, :], wdv[:, fsl, :], writes=[c.dWDs[fidx % 2][qd]])
    c.preconv.add(fidx)


def ln_bufs(c, fb):
    A = c.arena
    fb.G = A.alloc((D_MODEL,), F32)
    fb.B = A.alloc((D_MODEL,), F32)
    fb.d_GB = Dep()
    fb.rs = [A.alloc((D_MODEL,), F32) for _ in range(2)]
    fb.d_rs = [Dep(), Dep()]
    fb.sts = [A.alloc((20,), F32) for _ in range(2)]
    fb.d_sts = [Dep(), Dep()]
    fb.pend = []


def layer_norm_tile(c, tt):
    S = c.S
    fb = c.ffn_bufs
    X = c.X
    r, st = fb.rs[tt % 2], fb.sts[tt % 2]
    d_r, d_st = fb.d_rs[tt % 2], fb.d_sts[tt % 2]
    S.dve(lambda e: e.bn_aggr(out=st[:, 12:14], in_=st[:, 0:12]), reads=[d_st], writes=[d_st])
    S.act(lambda e: e.activation(out=st[:, 14:15], in_=st[:, 13:14], func=AF.Sqrt, bias=LN_EPS), reads=[d_st], writes=[d_st])
    S.dve(lambda e: e.reciprocal(out=st[:, 15:16], in_=st[:, 14:15]), reads=[d_st], writes=[d_st])
    S.dve(lambda e: e.scalar_tensor_tensor(out=st[:, 16:17], in0=st[:, 12:13], scalar=-1.0, in1=st[:, 15:16],
                                           op0=ALU.mult, op1=ALU.mult), reads=[d_st], writes=[d_st])
    S.act(lambda e: e.activation(out=r, in_=r, func=AF.Identity, scale=st[:, 15:16], bias=st[:, 16:17]),
          reads=[d_r, d_st], writes=[d_r])
    S.dve(lambda e: e.tensor_tensor(out=r, in0=r, in1=fb.G, op=ALU.mult), reads=[d_r, fb.d_GB], writes=[d_r])
    S.dve(lambda e: e.tensor_tensor(out=X[:, tt, :], in0=r, in1=fb.B, op=ALU.add), reads=[d_r, fb.d_GB], writes=[c.dX[tt]])
    fb.pend.append(tt)


def residual_half(c, tt, half, bk):
    S = c.S
    fb = c.ffn_bufs
    r, st = fb.rs[tt % 2], fb.sts[tt % 2]
    d_r, d_st = fb.d_rs[tt % 2], fb.d_sts[tt % 2]
    hs = slice(half * 512, (half + 1) * 512)
    S.dve(lambda e: e.scalar_tensor_tensor(out=r[:, hs], in0=c.X[:, tt, hs], scalar=ALPHA, in1=c.bank[bk][:],
                                           op0=ALU.mult, op1=ALU.add), reads=[c.dX[tt], c.dbank[bk]], writes=[d_r])
    S.dve(lambda e: e.bn_stats(out=st[:, half * 6:(half + 1) * 6], in_=r[:, hs]), reads=[d_r], writes=[d_st])


def flush_transposes(c, keep=0):
    fb = c.ffn_bufs
    while len(fb.pend) > keep:
        emit_transposes(c, fb.pend.pop(0))


def next_bank(c, pool=None):
    if pool is None:
        c.bank_i = (getattr(c, "bank_i", -1) + 1) % 8
        return c.bank_i
    key = "bank_i_%d_%d" % (pool[0], pool[1])
    i = getattr(c, key, -1) + 1
    setattr(c, key, i)
    return pool[0] + i % (pool[1] - pool[0])


def mm_group(c, bk, out_ap, pairs, reads, start=True, stop=True):
    S = c.S
    n = len(pairs)
    for i, (lt, rh) in enumerate(pairs):
        S.pe(lambda e, lt=lt, rh=rh, i=i: e.matmul(out_ap, lhsT=lt, rhs=rh, start=(start and i == 0),
                                                  stop=(stop and i == n - 1)),
             reads=reads, writes=[c.dbank[bk]], inc=(i == n - 1))


def mixer(c, l):
    phase_P(c, l)
    if c.stage == "P":
        return
    phase_C(c, l)
    if c.stage == "C":
        return
    phase_G(c, l)
    if c.stage[0] == "G":
        return
    phase_N(c, l)
    if c.stage[0] == "N":
        return
    phase_O(c, l)


def phase_P(c, l):
    S, I, XT = c.S, c.I, c.XT
    S.barrier()
    A = c.arena
    A.reset()
    wfm = [A.alloc((KC, 512), BF16) for _ in range(2)]
    d_wfm = [Dep(), Dep()]
    stgf = [A.alloc((4, 512), F32) for _ in range(2)]
    stgb = [A.alloc((4, 512), BF16) for _ in range(2)]
    d_stg = [Dep(), Dep()]
    wtm = A.alloc((KC, 544), BF16)
    d_wtm = Dep()
    stt = [A.alloc((544,), F32) for _ in range(2)]
    d_stt = [Dep(), Dep()]
    wv = I["w_in"][l].rearrange("(k p) n -> p k n", p=128)
    HTv = c.HT.rearrange("(c p) t -> p c t", p=128)
    HBv = c.HB.rearrange("(c p) t -> p c t", p=128)
    groups = [(0, 4, "T", 0), (512, 4, "T", 4), (1024, 4, "T", 8), (2312, 2, "T", 12),
              (1800, 4, "B", 0), (2568, 1, "B", 4), (2824, 1, "B", 5)]
    gi = 0
    si = 0
    ev = 0
    for (col0, nch, dest, ch0) in groups:
        w = wfm[gi % 2]
        dw = d_wfm[gi % 2]
        gi += 1
        S.dma("pool", w[:, :, 0:nch * 128], wv[:, :, col0:col0 + nch * 128], writes=[dw])
        for tb in range(4):
            stg = (stgf if dest == "T" else stgb)[si % 2]
            dstg = d_stg[si % 2]
            si += 1
            for ci in range(nch):
                bk = next_bank(c)
                mm_group(c, bk, c.bank[bk][:],
                         [(w[:, k, ci * 128:(ci + 1) * 128], XT[:, k, tb * 512:(tb + 1) * 512]) for k in range(KC)],
                         reads=[dw] + [c.dXT[tb * 4 + i] for i in range(4)])
                if ev % 2 == 0:
                    S.act(lambda e, stg=stg, ci=ci, bk=bk: e.activation(out=stg[:, ci, :], in_=c.bank[bk][:], func=AF.Copy),
                          reads=[c.dbank[bk]], writes=[dstg])
                else:
                    S.dve(lambda e, stg=stg, ci=ci, bk=bk: e.tensor_copy(out=stg[:, ci, :], in_=c.bank[bk][:]),
                          reads=[c.dbank[bk]], writes=[dstg])
                ev += 1
            dv, dd = (HTv, c.dHT) if dest == "T" else (HBv, c.dHB)
            S.dma("sp", dv[:, ch0:ch0 + nch, tb * 512:(tb + 1) * 512], stg[:, 0:nch, :],
                  reads=[dstg], writes=[dd[ch0 + ci][tb] for ci in range(nch)])
    for (c0, n, o) in [(1536, 264, 0), (2696, 128, 264), (2952, 128, 392), (3080, 24, 520)]:
        S.dma("pool", wtm[:, :, o:o + n], wv[:, :, c0:c0 + n], writes=[d_wtm])
    for tt in range(NT):
        st = stt[tt % 2]
        dst = d_stt[tt % 2]
        for (o, n) in ((0, 264), (264, 280)):
            bk = next_bank(c)
            mm_group(c, bk, c.bank[bk][:, 0:n],
                     [(XT[:, k, tt * 128:(tt + 1) * 128], wtm[:, k, o:o + n]) for k in range(KC)],
                     reads=[d_wtm, c.dXT[tt]])
            if o == 0:
                S.act(lambda e, st=st, bk=bk, o=o, n=n: e.activation(out=st[:, o:o + n], in_=c.bank[bk][:, 0:n], func=AF.Copy),
                      reads=[c.dbank[bk]], writes=[dst])
            else:
                S.dve(lambda e, st=st, bk=bk, o=o, n=n: e.tensor_copy(out=st[:, o:o + n], in_=c.bank[bk][:, 0:n]),
                      reads=[c.dbank[bk]], writes=[dst])
        S.pool(lambda e, st=st, tt=tt: e.tensor_copy(out=c.ab_sb[:, tt, :], in_=st[:, 256:264]),
               reads=[dst], writes=[c.d_ab])
        S.pool(lambda e, st=st, tt=tt: e.tensor_copy(out=c.gt_sb[:, tt, :], in_=st[:, 520:544]),
               reads=[dst], writes=[c.d_gt])
        S.dma("sp", c.HK[tt * 128:(tt + 1) * 128, :], st[:], reads=[dst], writes=[c.dHK[tt]])


def phase_C(c, l):
    S, I, XT = c.S, c.I, c.XT
    S.barrier()
    A = c.arena
    A.reset()
    cw = A.alloc((2, 3), F32)
    d_cw = Dep()
    S.dma("sp", cw, I["conv_w"][l].rearrange("(c p) k -> p c k", p=128), writes=[d_cw])
    cbt = A.alloc((SEQ,), F32)
    cct = A.alloc((SEQ,), F32)
    uu = A.alloc((SEQ + 2,), F32)
    acc = A.alloc((SEQ,), F32)
    d_cb, d_cc, d_uu, d_acc = Dep(), Dep(), Dep(), Dep()
    HTv = c.HT.rearrange("(c p) t -> p c t", p=128)
    for ch in range(2):
        S.dma("sp", cbt, HTv[:, 0 + ch, :], reads=c.dHT[0 + ch], writes=[d_cb])
        S.dma("sp", cct, HTv[:, 2 + ch, :], reads=c.dHT[2 + ch], writes=[d_cc])
        S.pool(lambda e: e.memset(uu[:, 0:2], 0.0), writes=[d_uu])
        S.dma("sp", uu[:, 2:SEQ + 2], HTv[:, 4 + ch, :], reads=c.dHT[4 + ch], writes=[d_uu])
        S.dve(lambda e: e.tensor_tensor(out=uu[:, 2:SEQ + 2], in0=uu[:, 2:SEQ + 2], in1=cct, op=ALU.mult),
              reads=[d_uu, d_cc], writes=[d_uu])
        S.dve(lambda e, ch=ch: e.tensor_scalar(out=acc, in0=uu[:, 0:SEQ], scalar1=cw[:, ch, 0:1], scalar2=None,
                                               op0=ALU.mult), reads=[d_uu, d_cw], writes=[d_acc])
        for j in (1, 2):
            S.dve(lambda e, ch=ch, j=j: e.scalar_tensor_tensor(out=acc, in0=uu[:, j:SEQ + j], scalar=cw[:, ch, j:j + 1],
                                                               in1=acc, op0=ALU.mult, op1=ALU.add),
                  reads=[d_uu, d_cw, d_acc], writes=[d_acc])
        S.dve(lambda e, ch=ch: e.tensor_tensor(out=XT[:, ch, :], in0=cbt, in1=acc, op=ALU.mult),
              reads=[d_cb, d_acc], writes=list(c.dXT))


def phase_G(c, l):
    S, I, XT = c.S, c.I, c.XT
    idf = c.ident_f
    S.barrier()
    A = c.arena
    A.reset()
    HTv = c.HT.rearrange("(c p) t -> p c t", p=128)
    gw = A.alloc((6, 4), F32)
    d_gw = Dep()
    S.dma("sp", gw, I["gdn_conv_w"][l].rearrange("(c p) k -> p c k", p=128), writes=[d_gw])
    uu = A.alloc((SEQ + 3,), F32)
    acc = A.alloc((SEQ,), F32)
    cvo = A.alloc((SEQ,), F32)
    d_uu, d_acc, d_cvo = Dep(), Dep(), Dep()
    QCv = c.QC.rearrange("(c p) t -> p c t", p=128)
    for ch in range(6):
        S.pool(lambda e: e.memset(uu[:, 0:3], 0.0), writes=[d_uu])
        S.dma("sp", uu[:, 3:SEQ + 3], HTv[:, 6 + ch, :], reads=c.dHT[6 + ch], writes=[d_uu])
        S.dve(lambda e, ch=ch: e.tensor_scalar(out=acc, in0=uu[:, 0:SEQ], scalar1=gw[:, ch, 0:1], scalar2=None,
                                               op0=ALU.mult), reads=[d_uu, d_gw], writes=[d_acc])
        for j in (1, 2, 3):
            S.dve(lambda e, ch=ch, j=j: e.scalar_tensor_tensor(out=acc, in0=uu[:, j:SEQ + j], scalar=gw[:, ch, j:j + 1],
                                                               in1=acc, op0=ALU.mult, op1=ALU.add),
                  reads=[d_uu, d_gw, d_acc], writes=[d_acc])
        S.act(lambda e: e.activation(out=cvo, in_=acc, func=AF.Silu), reads=[d_acc], writes=[d_cvo])
        S.dma("sp", QCv[:, ch, :], cvo, reads=[d_cvo], writes=[c.dQC[ch]])
    if c.stage == "G1":
        return
    S.barrier()
    A.reset()
    NCOL = NT * 4
    cst = A.alloc((6, 128), F32)
    d_cst = Dep()
    for i_, nm in enumerate(["bu", "he", "ho", "m1", "m2"]):
        S.dma("sp", cst[:, i_, :], I[nm][:, :], writes=[d_cst])
    RK = A.alloc((6, 256), F32)
    S.dma("sp", RK, I["rk"].rearrange("p (k n) -> p k n", k=6), writes=[d_cst])
    I4 = A.alloc((2, 128), F32)
    for h_ in range(2):
        S.dma("sp", I4[:, h_, :], I["ident"][:, :], writes=[d_cst])
    S.dve(lambda e: e.memset(cst[:, 5, :], 1.0), writes=[d_cst])
    BU, HE, HO, M1, M2, ONES = (cst[:, i_, :] for i_ in range(6))
    M12 = cst[:, 3:5, :].rearrange("p a b -> p (a b)")
    hp = A.alloc((3, 4), F32)
    d_hp = Dep()
    S.dma("sp", hp[:, 0, :], I["gdn_dt_bias"][l:l + 1, :].broadcast_to([128, 4]), writes=[d_hp])
    S.dma("sp", hp[:, 1, :], I["gdn_a_log"][l:l + 1, :].broadcast_to([128, 4]), writes=[d_hp])
    S.act(lambda e: e.activation(out=hp[:, 2, :], in_=hp[:, 1, :], func=AF.Exp), reads=[d_hp], writes=[d_hp])
    S.dve(lambda e: e.tensor_scalar(out=hp[:, 2, :], in0=hp[:, 2, :], scalar1=-1.0, scalar2=None, op0=ALU.mult),
          reads=[d_hp], writes=[d_hp])
    nw = A.alloc((64,), F32)
    S.dma("sp", nw, I["gdn_norm_w"][l:l + 1, :].broadcast_to([128, 64]), writes=[d_hp])
    sc = A.alloc((12, NCOL), F32)
    d_sc = Dep()
    G_, GC, NGC, EGC, GLE, GLO, EGLE, EGLO, KDS, BETA, NBETA, TMP = (sc[:, i_, :] for i_ in range(12))
    g3 = G_.rearrange("p (t h) -> p t h", h=4)
    t3 = TMP.rearrange("p (t h) -> p t h", h=4)
    b3 = BETA.rearrange("p (t h) -> p t h", h=4)
    S.act(lambda e: e.activation(out=b3, in_=c.ab_sb[:, :, 4:8], func=AF.Sigmoid), reads=[c.d_ab], writes=[d_sc])
    S.dve(lambda e: e.tensor_scalar(out=NBETA, in0=BETA, scalar1=-1.0, scalar2=None, op0=ALU.mult),
          reads=[d_sc], writes=[d_sc])
    for tt in range(NT):
        S.dve(lambda e, tt=tt: e.tensor_tensor(out=t3[:, tt, :], in0=c.ab_sb[:, tt, 0:4], in1=hp[:, 0, :], op=ALU.add),
              reads=[c.d_ab, d_hp], writes=[d_sc])
    S.act(lambda e: e.activation(out=TMP, in_=TMP, func=AF.Exp), reads=[d_sc], writes=[d_sc])
    S.act(lambda e: e.activation(out=TMP, in_=TMP, func=AF.Ln, bias=1.0), reads=[d_sc], writes=[d_sc])
    for tt in range(NT):
        S.dve(lambda e, tt=tt: e.tensor_tensor(out=g3[:, tt, :], in0=t3[:, tt, :], in1=hp[:, 2, :], op=ALU.mult),
              reads=[d_sc, d_hp], writes=[d_sc])
    bk = next_bank(c)
    for i_, (m, dst) in enumerate(((BU, GC), (HE, GLE), (HO, GLO))):
        mm_group(c, bk, c.bank[bk][:, i_ * NCOL:(i_ + 1) * NCOL], [(m, G_)], reads=[d_cst, d_sc])
    for i_, dst in enumerate((GC, GLE, GLO)):
        S.dve(lambda e, i_=i_, dst=dst, bk=bk: e.tensor_copy(out=dst, in_=c.bank[bk][:, i_ * NCOL:(i_ + 1) * NCOL]),
              reads=[c.dbank[bk]], writes=[d_sc])
    S.dve(lambda e: e.tensor_scalar(out=NGC, in0=GC, scalar1=-1.0, scalar2=None, op0=ALU.mult), reads=[d_sc], writes=[d_sc])
    S.act(lambda e: e.activation(out=EGC, in_=GC, func=AF.Exp), reads=[d_sc], writes=[d_sc])
    S.act(lambda e: e.activation(out=EGLE, in_=GLE, func=AF.Exp), reads=[d_sc], writes=[d_sc])
    S.act(lambda e: e.activation(out=EGLO, in_=GLO, func=AF.Exp), reads=[d_sc], writes=[d_sc])
    S.dve(lambda e: e.tensor_tensor(out=KDS[0:64, :], in0=GLE[0:64, :], in1=GC[0:64, :], op=ALU.subtract),
          reads=[d_sc], writes=[d_sc])
    S.dve(lambda e: e.tensor_tensor(out=KDS[64:128, :], in0=GLO[64:128, :], in1=GC[64:128, :], op=ALU.subtract),
          reads=[d_sc], writes=[d_sc])
    S.act(lambda e: e.activation(out=KDS, in_=KDS, func=AF.Exp), reads=[d_sc], writes=[d_sc])

    if c.stage == "G2":
        return
    NH = 2
    DEPTH = 4
    St = A.alloc((4, 64), F32)
    Stb = A.alloc((4, 64), BF16)
    d_St = [Dep(), Dep()]
    S.dve(lambda e: e.memset(St[:], 0.0), writes=d_St)
    S.dve(lambda e: e.memset(Stb[:], 0.0), writes=d_St)

    def make_set():
        B = Ctx()
        B.tok = A.alloc((NH, 3, 128), F32)
        B.d_tok = Dep()
        B.ss = A.alloc((NH, 4), F32)
        B.d_ss = Dep()
        B.fT = A.alloc((NH, 384), BF16)
        B.tokb = A.alloc((NH, 3, 128), BF16)
        B.Pb = A.alloc((NH, 128), BF16)
        B.d_fT = Dep()
        B.FD = A.alloc((2, NH, 256), F32)
        B.d_FD = [Dep(), Dep()]
        B.MN = A.alloc((NH, 256), F32)
        B.d_MN = Dep()
        B.XY = A.alloc((2, NH, 128), F32)
        B.d_XY = [Dep(), Dep()]
        B.aqk = A.alloc((NH, 128), BF16)
        B.d_aqk = Dep()
        B.PP = [A.alloc((NH, 128), F32) for _ in range(2)]
        B.d_PP = [Dep(), Dep()]
        B.u_sb = A.alloc((NH, 64), F32)
        B.wT_sb = A.alloc((NH, 128), BF16)
        B.d_uw = Dep()
        B.vn = [A.alloc((NH, 64), BF16) for _ in range(2)]
        B.d_vn = Dep()
        B.o_sb = A.alloc((NH, 64), F32)
        B.d_o = Dep()
        B.zt = A.alloc((NH * 64,), F32)
        B.d_z = Dep()
        B.osq = A.alloc((NH, 64), F32)
        B.rst = A.alloc((8,), F32)
        B.d_rst = Dep()
        B.yb = A.alloc((NH * 64,), BF16)
        B.d_yb = Dep()
        B.qkt = B.XY[:].rearrange("p a b c -> p (a b c)")[:, 0:384].rearrange("p (a b) -> p a b", a=3)
        S.dve(lambda e: e.memset(B.fT[:], 0.0), writes=[B.d_fT])
        S.dve(lambda e: e.memset(B.tok[:], 0.0), writes=[B.d_tok])
        S.dve(lambda e: e.memset(B.tokb[:], 0.0), writes=[B.d_tok])
        S.dve(lambda e: e.memset(B.wT_sb[:], 0.0), writes=[B.d_uw])
        S.dve(lambda e: e.memset(B.vn[0][:], 0.0), writes=[B.d_vn])
        S.dve(lambda e: e.memset(B.vn[1][:], 0.0), writes=[B.d_vn])
        return B

    free_sets = [make_set() for _ in range(DEPTH)]
    h_done = set()
    HR = range(NH)

    def unit(tt, hp, B):
        tsl = slice(tt * 128, (tt + 1) * 128)
        cols = [tt * 4 + hp * 2 + h for h in HR]
        tok, ss, fT, FD, MN, XY, aqk, PP = B.tok, B.ss, B.fT, B.FD, B.MN, B.XY, B.aqk, B.PP
        d_tok, d_ss, d_fT, d_FD, d_MN, d_XY, d_aqk, d_PP = B.d_tok, B.d_ss, B.d_fT, B.d_FD, B.d_MN, B.d_XY, B.d_aqk, B.d_PP
        Fm, Dg = FD[:, 0], FD[:, 1]
        d_F, d_Dg = d_FD[0], d_FD[1]
        S.dma("sp", B.zt, c.HK[tsl, hp * 128:(hp + 1) * 128], reads=[c.dHK[tt]], writes=[B.d_z])
        for j, chn in enumerate((hp, 2 + hp, 4 + hp)):
            S.dma("sp", B.qkt[:, j, :], QCv[:, chn, tsl], reads=[c.dQC[chn]], writes=list(d_XY))
        yield
        bA = next_bank(c)
        for j in range(3):
            S.pe(lambda e, j=j: e.transpose(out=c.bank[bA][:, j * 128:(j + 1) * 128], in_=B.qkt[:, j, :], identity=idf[:]),
                 reads=list(d_XY) + [c.d_ident], writes=[c.dbank[bA]], inc=(j == 2))
        for h in HR:
            for j in range(2):
                S.act(lambda e, j=j, h=h: e.activation(out=B.osq[:, 0, :], in_=c.bank[bA][:, h * 64 + j * 128:h * 64 + j * 128 + 64],
                                                       func=AF.Square, accum_out=ss[:, h, j:j + 1]),
                      reads=[c.dbank[bA]], writes=[d_ss, B.d_rst])
        S.dve(lambda e: e.tensor_scalar(out=ss[:, :, 2:4], in0=ss[:, :, 0:2], scalar1=NORM_EPS, scalar2=None, op0=ALU.add),
              reads=[d_ss], writes=[d_ss])
        S.act(lambda e: e.activation(out=ss[:, :, 2:4], in_=ss[:, :, 2:4], func=AF.Sqrt), reads=[d_ss], writes=[d_ss])
        S.dve(lambda e: e.reciprocal(out=ss[:, :, 2:4], in_=ss[:, :, 2:4]), reads=[d_ss], writes=[d_ss])
        for h in HR:
            col = cols[h]
            qp, kp, vp = (c.bank[bA][:, h * 64 + j * 128:h * 64 + j * 128 + 64] for j in range(3))
            rd = [c.dbank[bA], d_ss, d_sc]
            S.dve(lambda e, h=h, qp=qp: e.tensor_scalar(out=tok[:, h, 0, 0:64], in0=qp, scalar1=ss[:, h, 2:3], scalar2=0.125,
                                                        op0=ALU.mult, op1=ALU.mult), reads=rd, writes=[d_tok])
            S.dve(lambda e, h=h, kp=kp: e.tensor_scalar(out=tok[:, h, 2, 0:64], in0=kp, scalar1=ss[:, h, 3:4], scalar2=None,
                                                        op0=ALU.mult), reads=rd, writes=[d_tok])
            S.dve(lambda e, h=h, vp=vp, col=col: e.tensor_scalar(out=B.tokb[:, h, 2, 0:64], in0=vp, scalar1=BETA[:, col:col + 1],
                                                                 scalar2=None, op0=ALU.mult), reads=rd, writes=[d_tok])
        yield
        for h in HR:
            col = cols[h]
            S.pool(lambda e, h=h, col=col: e.tensor_scalar(out=tok[:, h, 1, 0:64], in0=tok[:, h, 0, 0:64], scalar1=EGC[:, col:col + 1],
                                                           scalar2=1.0, op0=ALU.mult, op1=ALU.mult), reads=[d_tok, d_sc], writes=[d_tok])
            S.pool(lambda e, h=h, col=col: e.tensor_scalar(out=B.tokb[:, h, 0, 0:64], in0=tok[:, h, 2, 0:64], scalar1=BETA[:, col:col + 1],
                                                           scalar2=EGC[:, col:col + 1], op0=ALU.mult, op1=ALU.mult),
                   reads=[d_tok, d_sc], writes=[d_tok])
            S.pool(lambda e, h=h, col=col: e.tensor_scalar(out=B.tokb[:, h, 1, 0:64], in0=tok[:, h, 2, 0:64], scalar1=KDS[:, col:col + 1],
                                                           scalar2=1.0, op0=ALU.mult, op1=ALU.mult), reads=[d_tok, d_sc], writes=[d_tok])
            S.pool(lambda e, h=h, col=col: e.tensor_scalar(out=Dg[:, h, 0:128], in0=idf[:], scalar1=NGC[:, col:col + 1],
                                                           scalar2=1.0, op0=ALU.mult, op1=ALU.mult), reads=[c.d_ident, d_sc], writes=[d_Dg])
            S.pool(lambda e, h=h, col=col: e.tensor_scalar(out=Dg[:, h, 128:256], in0=idf[:], scalar1=GC[:, col:col + 1],
                                                           scalar2=1.0, op0=ALU.mult, op1=ALU.mult), reads=[c.d_ident, d_sc], writes=[d_Dg])
        yield
        for h in HR:
            bk = next_bank(c)
            for j, src in enumerate((2, 0, 1)):
                S.pe(lambda e, j=j, src=src, h=h, bk=bk: e.transpose(out=c.bank[bk][:, j * 128:(j + 1) * 128], in_=tok[:, h, src, :],
                                                                     identity=idf[:]),
                     reads=[d_tok, c.d_ident], writes=[c.dbank[bk]], inc=(j == 2))
            S.act(lambda e, h=h, bk=bk: e.activation(out=fT[:, h, :], in_=c.bank[bk][:, 0:384], func=AF.Copy),
                  reads=[c.dbank[bk]], writes=[d_fT])
        yield
        bC, bD = next_bank(c), next_bank(c)
        for h in HR:
            col = cols[h]
            o0 = h * 256
            knT = fT[:, h, 0:128]
            mm_group(c, bC, c.bank[bC][:, o0:o0 + 256], [(knT, fT[:, h, 0:256])], reads=[d_fT])
            mm_group(c, bD, c.bank[bD][:, o0:o0 + 256], [(ONES, Dg[:, h, :]), (idf[:], M12)], reads=[d_cst, d_Dg, c.d_ident])
        for h in HR:
            col = cols[h]
            o0 = h * 256
            S.act(lambda e, h=h, o0=o0, col=col: e.activation(out=Fm[:, h, 0:128], in_=c.bank[bD][:, o0:o0 + 128],
                                                              func=AF.Exp, bias=GC[:, col:col + 1]),
                  reads=[c.dbank[bD], d_sc], writes=[d_F])
            S.act(lambda e, h=h, o0=o0, col=col: e.activation(out=Fm[:, h, 128:256], in_=c.bank[bD][:, o0 + 128:o0 + 256],
                                                              func=AF.Exp, bias=NGC[:, col:col + 1]),
                  reads=[c.dbank[bD], d_sc], writes=[d_F])
            S.dve(lambda e, h=h, o0=o0, col=col: e.scalar_tensor_tensor(
                out=MN[:, h, 0:128], in0=c.bank[bC][:, o0:o0 + 128], scalar=NBETA[:, col:col + 1], in1=Fm[:, h, 0:128],
                op0=ALU.mult, op1=ALU.mult), reads=[c.dbank[bC], d_F, d_sc], writes=[d_MN])
            S.dve(lambda e, h=h, o0=o0: e.tensor_tensor(out=aqk[:, h, :], in0=c.bank[bC][:, o0 + 128:o0 + 256],
                                                        in1=Fm[:, h, 128:256], op=ALU.mult),
                  reads=[c.dbank[bC], d_F], writes=[d_aqk])
        yield
        bk = next_bank(c)
        for h in HR:
            S.pe(lambda e, h=h, bk=bk: e.transpose(out=c.bank[bk][:, h * 128:(h + 1) * 128], in_=MN[:, h, 0:128], identity=idf[:]),
                 reads=[d_MN, c.d_ident], writes=[c.dbank[bk]], inc=(h == NH - 1))
        for h in HR:
            S.act(lambda e, h=h, bk=bk: e.activation(out=MN[:, h, 128:256], in_=c.bank[bk][:, h * 128:(h + 1) * 128], func=AF.Copy),
                  reads=[c.dbank[bk]], writes=[d_MN])
        Dm, Um = PP[0], PP[1]
        W2 = NH * 128
        cc0 = FD[:, 0]
        for h in HR:
            S.pool(lambda e, h=h: e.tensor_tensor(out=cc0[:, h, :], in0=MN[:, h, :], in1=RK[:, 0, :], op=ALU.mult),
                   reads=[d_MN, d_cst], writes=[d_FD[0]])
        S.dve(lambda e: e.tensor_tensor(out=Dm[:], in0=cc0[:, :, 0:128], in1=I4[:, 0:NH, :], op=ALU.add),
              reads=[d_FD[0], d_cst], writes=[d_PP[0]])
        S.dve(lambda e: e.tensor_tensor(out=Um[:], in0=cc0[:, :, 128:256], in1=I4[:, 0:NH, :], op=ALU.add),
              reads=[d_FD[0], d_cst], writes=[d_PP[1]])
        yield
        for k in range(1, 6):
            cc, d_cc = FD[:, k % 2], d_FD[k % 2]
            for h in HR:
                S.pool(lambda e, h=h, k=k, cc=cc: e.tensor_tensor(out=cc[:, h, 0:128], in0=MN[:, h, 0:128], in1=RK[:, k, 0:128], op=ALU.mult),
                       reads=[d_MN, d_cst], writes=[d_cc])
            by = next_bank(c)
            for h in HR:
                mm_group(c, by, c.bank[by][:, h * 128:(h + 1) * 128], [(cc[:, h, 0:128], Um[:, h, :])], reads=[d_cc, d_PP[1]])
            S.act(lambda e, by=by: e.activation(out=XY[:, 1].rearrange("p a b -> p (a b)"), in_=c.bank[by][:, 0:W2], func=AF.Copy),
                  reads=[c.dbank[by]], writes=[d_XY[1]])
            yield
            bu = next_bank(c)
            for h in HR:
                mm_group(c, bu, c.bank[bu][:, h * 128:(h + 1) * 128], [(Dm[:, h, :], XY[:, 1, h, :])], reads=[d_PP[0], d_XY[1]])
            S.dve(lambda e, bu=bu: e.tensor_tensor(out=Um[:].rearrange("p a b -> p (a b)"), in0=c.bank[bu][:, 0:W2],
                                                   in1=Um[:].rearrange("p a b -> p (a b)"), op=ALU.add),
                  reads=[c.dbank[bu], d_PP[1]], writes=[d_PP[1]])
            yield
            if k < 5:
                bd = next_bank(c)
                for h in HR:
                    S.pe(lambda e, h=h, bd=bd: e.transpose(out=c.bank[bd][:, h * 128:(h + 1) * 128], in_=Um[:, h, :], identity=idf[:]),
                         reads=[d_PP[1], c.d_ident], writes=[c.dbank[bd]], inc=(h == NH - 1))
                S.act(lambda e, bd=bd: e.activation(out=Dm[:].rearrange("p a b -> p (a b)"), in_=c.bank[bd][:, 0:W2], func=AF.Copy),
                      reads=[c.dbank[bd]], writes=[d_PP[0]])
                yield
        S.act(lambda e: e.activation(out=B.Pb[:], in_=Um[:], func=AF.Copy), reads=[d_PP[1]], writes=[d_PP[1]])
        Pf, d_Pf = B.Pb, d_PP[1]
        bU, bW = next_bank(c), next_bank(c)
        for h in HR:
            mm_group(c, bU, c.bank[bU][:, h * 64:(h + 1) * 64], [(Pf[:, h, :], B.tokb[:, h, 2, 0:64])], reads=[d_Pf, d_tok])
            mm_group(c, bW, c.bank[bW][:, h * 128:(h + 1) * 128], [(B.tokb[:, h, 0, :], Pf[:, h, :])], reads=[d_Pf, d_tok])
        S.act(lambda e: e.activation(out=B.u_sb[:].rearrange("p a b -> p (a b)"), in_=c.bank[bU][:, 0:NH * 64], func=AF.Copy),
              reads=[c.dbank[bU]], writes=[B.d_uw])
        S.dve(lambda e: e.tensor_copy(out=B.wT_sb[:].rearrange("p a b -> p (a b)"), in_=c.bank[bW][:, 0:NH * 128]),
              reads=[c.dbank[bW]], writes=[B.d_uw])
        yield
        dS = d_St[hp]
        while tt > 0 and (tt - 1, hp) not in h_done:
            yield
        for pr in range(2):
            r0 = pr * 64
            rows = slice(r0, r0 + 64)
            EGL = EGLE if pr == 0 else EGLO
            vnp = B.vn[pr]
            bws = next_bank(c)
            for h in HR:
                hg = hp * 2 + h
                mm_group(c, bws, c.bank[bws][:, h * 64:(h + 1) * 64], [(B.wT_sb[:, h, :], Stb[:, hg, :])], reads=[B.d_uw, dS])
            S.dve(lambda e, rows=rows, bws=bws, vnp=vnp: e.tensor_tensor(out=vnp[rows].rearrange("p a b -> p (a b)"),
                                                                         in0=B.u_sb[rows].rearrange("p a b -> p (a b)"),
                                                                         in1=c.bank[bws][rows, 0:NH * 64], op=ALU.subtract),
                  reads=[B.d_uw, c.dbank[bws]], writes=[B.d_vn])
            yield
            bo, bs = next_bank(c), next_bank(c)
            for h in HR:
                hg = hp * 2 + h
                mm_group(c, bo, c.bank[bo][:, h * 64:(h + 1) * 64],
                         [(fT[:, h, 256:384], Stb[:, hg, :]), (aqk[:, h, :], vnp[:, h, :])],
                         reads=[d_fT, dS, d_aqk, B.d_vn])
                mm_group(c, bs, c.bank[bs][:, h * 64:(h + 1) * 64], [(B.tokb[:, h, 1, :], vnp[:, h, :])], reads=[d_tok, B.d_vn])
            S.act(lambda e, rows=rows, bo=bo: e.activation(out=B.o_sb[rows].rearrange("p a b -> p (a b)"), in_=c.bank[bo][rows, 0:NH * 64],
                                                           func=AF.Copy), reads=[c.dbank[bo]], writes=[B.d_o])
            for h in HR:
                hg = hp * 2 + h
                col = cols[h]
                S.dve(lambda e, h=h, hg=hg, col=col, bs=bs, EGL=EGL: e.scalar_tensor_tensor(
                    out=St[0:64, hg, :], in0=St[0:64, hg, :], scalar=EGL[0:64, col:col + 1], in1=c.bank[bs][0:64, h * 64:(h + 1) * 64],
                    op0=ALU.mult, op1=ALU.add), reads=[dS, c.dbank[bs], d_sc], writes=[dS])
            S.act(lambda e: e.activation(out=Stb[0:64, hp * 2:hp * 2 + 2, :], in_=St[0:64, hp * 2:hp * 2 + 2, :], func=AF.Copy),
                  reads=[dS], writes=[dS])
            yield
        h_done.add((tt, hp))
        o_sb, rst, zt, yb = B.o_sb, B.rst, B.zt, B.yb
        S.pool(lambda e: e.tensor_tensor(out=B.osq[:], in0=o_sb[:], in1=o_sb[:], op=ALU.mult), reads=[B.d_o], writes=[B.d_rst])
        S.dve(lambda e: e.tensor_reduce(out=rst[:, 0:NH], in_=B.osq[:], op=ALU.add, axis=AX.X), reads=[B.d_rst], writes=[B.d_rst])
        S.dve(lambda e: e.tensor_scalar(out=rst[:, 0:NH], in0=rst[:, 0:NH], scalar1=1.0 / 64, scalar2=NORM_EPS, op0=ALU.mult,
                                        op1=ALU.add), reads=[B.d_rst], writes=[B.d_rst])
        S.act(lambda e: e.activation(out=rst[:, 0:NH], in_=rst[:, 0:NH], func=AF.Sqrt), reads=[B.d_rst], writes=[B.d_rst])
        S.dve(lambda e: e.reciprocal(out=rst[:, 4:4 + NH], in_=rst[:, 0:NH]), reads=[B.d_rst], writes=[B.d_rst])
        S.act(lambda e: e.activation(out=zt, in_=zt, func=AF.Silu), reads=[B.d_z], writes=[B.d_z])
        for h in HR:
            S.dve(lambda e, h=h: e.scalar_tensor_tensor(out=o_sb[:, h, :], in0=o_sb[:, h, :], scalar=rst[:, 4 + h:5 + h], in1=nw,
                                                        op0=ALU.mult, op1=ALU.mult), reads=[B.d_o, B.d_rst, d_hp], writes=[B.d_o])
        S.dve(lambda e: e.tensor_tensor(out=yb, in0=o_sb[:].rearrange("p a b -> p (a b)"), in1=zt, op=ALU.mult),
              reads=[B.d_o, B.d_z], writes=[B.d_yb])
        bk = next_bank(c)
        pt = c.bank[bk][:].bitcast(BF16)
        S.pe(lambda e: e.transpose(out=pt[:, 0:128], in_=yb[:, 0:128], identity=c.ident_b[:]),
             reads=[B.d_yb, c.d_ident], writes=[c.dbank[bk]])
        S.act(lambda e: e.activation(out=XT[:, 2 + hp, tsl], in_=pt[:, 0:128], func=AF.Copy), reads=[c.dbank[bk]], writes=[c.dXT[tt]])

    units = [(tt, hp) for tt in range(NT) for hp in range(2)]
    if c.stage == "GX":
        units = units[:2]
    limit = None
    if c.stage.startswith("GY"):
        units = units[:1]
        limit = int(c.stage[2:])
    active = []
    ui = 0
    while ui < len(units) or active:
        if ui < len(units) and free_sets:
            Bs = free_sets.pop(0)
            active.append([unit(units[ui][0], units[ui][1], Bs), Bs, 0])
            ui += 1
        for item in list(active):
            try:
                if limit is not None and item[2] >= limit:
                    raise StopIteration
                item[2] += 1
                next(item[0])
            except StopIteration:
                active.remove(item)
                free_sets.append(item[1])


MNEG = -30000.0


def phase_N(c, l):
    S, I, XT = c.S, c.I, c.XT
    idf, idb = c.ident_f, c.ident_b
    S.barrier()
    A = c.arena
    A.reset()
    HTv = c.HT
    HBv = c.HB
    HKv = c.HK.rearrange("(t p) n -> p t n", p=128)
    preconvert_ffn(c, l, 2)
    QT = [A.alloc((NT, 4, 128), BF16) for _ in range(2)]
    KS = [A.alloc((SEQ,), BF16) for _ in range(2)]
    KW = [A.alloc((SEQ,), BF16) for _ in range(2)]
    VS = A.alloc((NT, 2, 65), BF16)
    VW = A.alloc((NT, 2, 65), BF16)
    KCc = [A.alloc((128,), BF16) for _ in range(2)]
    VCa = A.alloc((2, 97), F32)
    CM = A.alloc((NT, 128), BF16)
    FB = A.alloc((NT, 32), F32)
    EX = A.alloc((NT, 128), BF16)
    CAUS = A.alloc((128,), BF16)
    WINM = A.alloc((128,), BF16)
    ZB = A.alloc((512,), BF16)
    GT = A.alloc((NT, 24), F32)
    d_q, d_k, d_v, d_kc, d_vc, d_cst, d_gtt = Dep(), Dep(), Dep(), Dep(), Dep(), Dep(), Dep()
    for hk in range(2):
        S.dve(lambda e, hk=hk: e.memset(QT[hk][64:128], 0.0), writes=[d_q])
        S.dve(lambda e, hk=hk: e.memset(KS[hk][64:128], 0.0), writes=[d_k])
        S.dve(lambda e, hk=hk: e.memset(KW[hk][64:128], 0.0), writes=[d_k])
        S.dve(lambda e, hk=hk: e.memset(KCc[hk][:], 0.0), writes=[d_kc])
    for g in range(4):
        for hk in range(2):
            r0 = hk * 256 + g * 64
            S.dma("sp", QT[hk][0:64, :, g, :], HBv[r0:r0 + 64, :].rearrange("p (t k) -> p t k", k=128),
                  reads=sum((c.dHB[ch] for ch in range(4)), []), writes=[d_q])
    for hk in range(2):
        S.dma("sp", KS[hk][0:64], HBv[512 + hk * 64:576 + hk * 64, :], reads=c.dHB[4], writes=[d_k])
        S.dma("sp", KW[hk][0:64], HBv[640 + hk * 64:704 + hk * 64, :], reads=c.dHB[5], writes=[d_k])
    S.dve(lambda e: e.memset(VS[:], 1.0), writes=[d_v])
    S.dve(lambda e: e.memset(VW[:], 1.0), writes=[d_v])
    for hk in range(2):
        S.dma("pool", VS[:, :, hk, 0:64], HKv[:, :, 264 + hk * 64:328 + hk * 64], reads=list(c.dHK), writes=[d_v])
        S.dma("pool", VW[:, :, hk, 0:64], HKv[:, :, 392 + hk * 64:456 + hk * 64], reads=list(c.dHK), writes=[d_v])
    S.dma("pool", CM[0:127], I["cm"].rearrange("p (t k) -> p t k", k=128), writes=[d_cst])
    S.dma("sp", FB, I["fb"].rearrange("p (t k) -> p t k", k=32), writes=[d_cst])
    S.dve(lambda e: e.memset(EX[:], 0.0), writes=[d_cst])
    S.dma("pool", EX[0:32], I["ex"].rearrange("p (t k) -> p t k", k=128), writes=[d_cst])
    S.dma("pool", CAUS, I["caus"][:, :], writes=[d_cst])
    S.dma("pool", WINM, I["win"][:, :], writes=[d_cst])
    S.dve(lambda e: e.memset(ZB, 0.0), writes=[d_cst])
    S.act(lambda e: e.activation(out=GT[:], in_=c.gt_sb[:], func=AF.Sigmoid), reads=[c.d_gt], writes=[d_gtt])
    S.dve(lambda e: e.memset(VCa[:], 1.0), writes=[d_vc])
    for hk in range(2):
        S.dma("sp", VCa[0:127, hk, 65:97], I["ov"][:, :], writes=[d_vc])
    mark = A.off
    kT2 = A.alloc((SEQ,), F32)
    w1 = A.alloc((16, 128), F32)
    w2p = A.alloc((2, 128), F32)
    w2v = A.alloc((64,), F32)
    pe16 = A.alloc((128,), F32)
    pes = A.alloc((16,), F32)
    bias = A.alloc((2,), F32)
    hid = [A.alloc((128,), F32) for _ in range(2)]
    d_kT2, d_w1, d_w2, d_pe, d_bias = Dep(), Dep(), Dep(), Dep(), Dep()
    d_hid = [Dep(), Dep()]
    for which, chunk in (("k", 12), ("v", 13)):
        S.dma("sp", w1, I["cmp_%s_w1" % which][l].rearrange("(c p) h -> p c h", p=128), writes=[d_w1])
        if which == "k":
            S.dma("sp", w2v, I["cmp_k_w2"][l], writes=[d_w2])
        else:
            S.dma("sp", w2v, I["cmp_v_w2"][l], writes=[d_w2])
        S.dma("sp", pe16[0:16], I["cmp_pe_%s" % which][l].rearrange("(c q) d -> c (q d)", q=2), writes=[d_pe])
        bk = next_bank(c, (2, 8))
        mm_group(c, bk, c.bank[bk][:, 0:16], [(pe16[0:16, :], idf[0:16, 0:16])], reads=[d_pe, c.d_ident])
        S.dve(lambda e, bk=bk: e.tensor_copy(out=pes, in_=c.bank[bk][:, 0:16]), reads=[c.dbank[bk]], writes=[d_pe])
        bk = next_bank(c, (2, 8))
        mm_group(c, bk, c.bank[bk][:, 0:1], [(w1[:, lp, :], pes[:, lp:lp + 1]) for lp in range(16)], reads=[d_w1, d_pe])
        S.dve(lambda e, bk=bk: e.tensor_copy(out=bias[:, 0:1], in_=c.bank[bk][:, 0:1]), reads=[c.dbank[bk]], writes=[d_bias])
        for hk in range(2):
            rr = chunk * 128 + hk * 64
            S.dma("sp", kT2[0:64, :], HTv[rr:rr + 64, :], reads=c.dHT[chunk], writes=[d_kT2])
            S.dma("sp", kT2[64:128, 0:SEQ - 1], HTv[rr:rr + 64, 1:SEQ], reads=c.dHT[chunk], writes=[d_kT2])
            bk = next_bank(c, (2, 8))
            mm_group(c, bk, c.bank[bk][:, 0:127],
                     [(w1[:, lp, :], kT2[:, 2 * lp:2 * lp + 2017:16]) for lp in range(16)], reads=[d_w1, d_kT2])
            S.act(lambda e, bk=bk, hk=hk: e.activation(out=hid[hk][:, 0:127], in_=c.bank[bk][:, 0:127], func=AF.Silu,
                                                       bias=bias[:, 0:1]), reads=[c.dbank[bk], d_bias], writes=[d_hid[hk]])
        if which == "k":
            for hk in range(2):
                bk = next_bank(c, (2, 8))
                mm_group(c, bk, c.bank[bk][0:64, 0:127], [(w2v, hid[hk][:, 0:127])], reads=[d_w2, d_hid[hk]])
                S.dve(lambda e, bk=bk, hk=hk: e.tensor_copy(out=KCc[hk][0:64, 0:127], in_=c.bank[bk][0:64, 0:127]),
                      reads=[c.dbank[bk]], writes=[d_kc])
        else:
            bk = next_bank(c, (2, 8))
            for hk in range(2):
                mm_group(c, bk, c.bank[bk][0:127, hk * 64:(hk + 1) * 64], [(hid[hk][:, 0:127], w2v)], reads=[d_w2, d_hid[hk]])
            for hk in range(2):
                S.dve(lambda e, bk=bk, hk=hk: e.tensor_copy(out=VCa[0:127, hk, 0:64], in_=c.bank[bk][0:127, hk * 64:(hk + 1) * 64]),
                      reads=[c.dbank[bk]], writes=[d_vc])
    S.barrier()
    A.off = mark
    eT = [A.alloc((512,), F32) for _ in range(2)]
    d_eT = [Dep(), Dep()]
    pT = [A.alloc((512,), BF16) for _ in range(4)]
    d_pT = [Dep() for _ in range(4)]
    ycs = [A.alloc((512,), F32) for _ in range(2)]
    d_ycs = [[Dep(), Dep()], [Dep(), Dep()]]
    ycb = A.alloc((512,), BF16)
    d_ycb = Dep()
    smc = A.alloc((16,), F32)
    d_smc = Dep()
    smm = A.alloc((16,), F32)
    oTs = [A.alloc((512,), F32) for _ in range(2)]
    d_oTs = [Dep(), Dep()]
    oi = [0]
    pend_fin = []
    d_smm = Dep()
    imp = A.alloc((32,), F32)
    nsel = A.alloc((32,), BF16)
    d_imp = Dep()
    NSTs = [A.alloc((512,), BF16) for _ in range(2)]
    d_nsts = [Dep(), Dep()]
    for i_ in range(2):
        S.dve(lambda e, i_=i_: e.memset(NSTs[i_][:], 0.0), writes=[d_nsts[i_]])
    IDB4 = A.alloc((512,), BF16)
    CAUS4 = A.alloc((512,), BF16)
    WINM4 = A.alloc((512,), BF16)
    for g in range(4):
        S.dve(lambda e, g=g: e.tensor_copy(out=IDB4[:, g * 128:(g + 1) * 128], in_=idb[:]), reads=[c.d_ident], writes=[d_cst])
        S.dve(lambda e, g=g: e.tensor_copy(out=CAUS4[:, g * 128:(g + 1) * 128], in_=CAUS), reads=[d_cst], writes=[d_cst])
        S.dve(lambda e, g=g: e.tensor_copy(out=WINM4[:, g * 128:(g + 1) * 128], in_=WINM), reads=[d_cst], writes=[d_cst])
    ei = [0]
    pi = [0]

    def branch_evac(tt, hk, bo, width, br, first, sm, d_sm, yc, d_yc):
        ov = c.bank[bo][:, 0:4 * width].rearrange("p (g w) -> p g w", g=4)
        S.dve(lambda e: e.tensor_scalar(out=sm[:, 0:4], in0=ov[:, :, 64], scalar1=1e-30, scalar2=None, op0=ALU.max),
              reads=[c.dbank[bo]], writes=[d_sm])
        S.dve(lambda e: e.reciprocal(out=sm[:, 0:4], in_=sm[:, 0:4]), reads=[d_sm], writes=[d_sm])
        S.dve(lambda e: e.tensor_tensor(out=sm[:, 4:8], in0=sm[:, 0:4], in1=GT[:, tt, hk * 12 + br:hk * 12 + br + 10:3], op=ALU.mult),
              reads=[d_sm, d_gtt], writes=[d_sm])
        for g in range(4):
            hq = hk * 4 + g
            if first:
                S.dve(lambda e, g=g, hq=hq: e.tensor_scalar(out=yc[:, hq * 64:(hq + 1) * 64], in0=ov[:, g, 0:64],
                                                            scalar1=sm[:, 4 + g:5 + g], scalar2=None, op0=ALU.mult),
                      reads=[c.dbank[bo], d_sm], writes=[d_yc])
            else:
                S.dve(lambda e, g=g, hq=hq: e.scalar_tensor_tensor(out=yc[:, hq * 64:(hq + 1) * 64], in0=ov[:, g, 0:64],
                                                                   scalar=sm[:, 4 + g:5 + g], in1=yc[:, hq * 64:(hq + 1) * 64],
                                                                   op0=ALU.mult, op1=ALU.add),
                      reads=[c.dbank[bo], d_sm, d_yc], writes=[d_yc])
        return ov

    def cmp_gen(tt, hk, ui):
        NST, d_nst = NSTs[ui % 2], d_nsts[ui % 2]
        yc, d_yc = ycs[tt % 2], d_ycs[tt % 2][hk]
        qv = QT[hk][:, tt, :, :].rearrange("p g k -> p (g k)")
        bs = next_bank(c, (6, 8))
        mm_group(c, bs, c.bank[bs][0:127, :], [(KCc[hk][:, 0:127], qv)], reads=[d_kc, d_q], stop=False)
        for g in range(4):
            mm_group(c, bs, c.bank[bs][0:127, g * 128:(g + 1) * 128], [(idb[0:127, 0:127], CM[0:127, tt, :])],
                     reads=[c.d_ident, d_cst], start=False, stop=(g == 3))
        e_ = eT[ei[0] % 2]
        d_e = d_eT[ei[0] % 2]
        ei[0] += 1
        S.act(lambda e: e.activation(out=e_[0:127, :], in_=c.bank[bs][0:127, :], func=AF.Exp, scale=0.125),
              reads=[c.dbank[bs]], writes=[d_e])
        yield
        bo = 2
        for g in range(4):
            mm_group(c, bo, c.bank[bo][:, g * 97:(g + 1) * 97], [(e_[0:127, g * 128:(g + 1) * 128], VCa[0:127, hk, :])],
                     reads=[d_e, d_vc])
        yield
        ov = branch_evac(tt, hk, bo, 97, 0, True, smc, d_smc, yc, d_yc)
        yield
        for g in range(4):
            if g == 0:
                S.dve(lambda e: e.tensor_scalar(out=imp, in0=ov[:, 0, 65:97], scalar1=smc[:, 0:1], scalar2=None, op0=ALU.mult),
                      reads=[c.dbank[bo], d_smc], writes=[d_imp])
            else:
                S.dve(lambda e, g=g: e.scalar_tensor_tensor(out=imp, in0=ov[:, g, 65:97], scalar=smc[:, g:g + 1], in1=imp,
                                                            op0=ALU.mult, op1=ALU.add),
                      reads=[c.dbank[bo], d_smc, d_imp], writes=[d_imp])
        yield
        S.dve(lambda e: e.tensor_tensor(out=imp, in0=imp, in1=FB[:, tt, :], op=ALU.add), reads=[d_imp, d_cst], writes=[d_imp])
        S.dve(lambda e: e.max(out=smc[:, 8:16], in_=imp), reads=[d_imp], writes=[d_smc])
        yield
        S.dve(lambda e: e.tensor_scalar(out=imp, in0=imp, scalar1=smc[:, 15:16], scalar2=-MNEG, op0=ALU.is_ge, op1=ALU.mult),
              reads=[d_imp, d_smc], writes=[d_imp])
        S.dve(lambda e: e.tensor_scalar(out=nsel, in0=imp, scalar1=MNEG, scalar2=None, op0=ALU.add), reads=[d_imp], writes=[d_imp])
        yield
        bt = next_bank(c, (6, 8))
        mm_group(c, bt, c.bank[bt][0:32, :], [(nsel, IDB4)], reads=[d_imp, d_cst])
        S.act(lambda e: e.activation(out=NST[0:32, :], in_=c.bank[bt][0:32, :], func=AF.Copy), reads=[c.dbank[bt]], writes=[d_nst])

    def main_gen(tt, hk, ui):
        NST, d_nst = NSTs[ui % 2], d_nsts[ui % 2]
        yc, d_yc = ycs[tt % 2], d_ycs[tt % 2][hk]
        qv = QT[hk][:, tt, :, :].rearrange("p g k -> p (g k)")
        its = [(1, kb) for kb in range(0, tt + 1)] + [(2, kb) for kb in range(max(0, tt - 4), tt + 1)]
        bos = {1: 0, 2: 1}
        first_kb = {1: 0, 2: max(0, tt - 4)}
        fr = {}

        def front(i):
            br, kb = its[i]
            Kt = KS if br == 1 else KW
            bs = next_bank(c, (3, 6))
            extra = []
            if br == 1:
                extra.append((EX[:, kb, :], NST[:, :], [d_cst, d_nst]))
            if kb == tt:
                extra.append((idb[:], CAUS4, [c.d_ident, d_cst]))
            if br == 2 and kb == tt - 4:
                extra.append((idb[:], WINM4, [c.d_ident, d_cst]))
            mm_group(c, bs, c.bank[bs][:], [(Kt[hk][:, kb * 128:(kb + 1) * 128], qv)], reads=[d_k, d_q],
                     stop=(len(extra) == 0))
            for xi, (lt, rh, rds) in enumerate(extra):
                mm_group(c, bs, c.bank[bs][:], [(lt, rh)], reads=rds, start=False, stop=(xi == len(extra) - 1))
            p_ = pT[pi[0] % 4]
            d_p = d_pT[pi[0] % 4]
            pi[0] += 1
            S.act(lambda e: e.activation(out=p_, in_=c.bank[bs][:], func=AF.Exp, scale=0.125),
                  reads=[c.dbank[bs]], writes=[d_p])
            fr[i] = (p_, d_p)

        def back(i):
            br, kb = its[i]
            Vt = VS if br == 1 else VW
            p_, d_p = fr.pop(i)
            bo = bos[br]
            mm_group(c, bo, c.bank[bo][0:65, :], [(Vt[:, kb, hk, :], p_)], reads=[d_p, d_v],
                     start=(kb == first_kb[br]), stop=(kb == tt))
            for pf in pend_fin:
                pf[0] += 1
            while pend_fin and pend_fin[0][0] >= 3:
                pend_fin.pop(0)[1]()
            if kb == tt:
                while len(pend_fin) >= 2:
                    pend_fin.pop(0)[1]()
                oT_, d_oT = oTs[oi[0] % 2], d_oTs[oi[0] % 2]
                oi[0] += 1
                S.act(lambda e: e.activation(out=oT_[0:65, :], in_=c.bank[bo][0:65, :], func=AF.Copy),
                      reads=[c.dbank[bo]], writes=[d_oT])

                def fin(oT_=oT_, d_oT=d_oT, br=br):
                    bt = next_bank(c, (3, 6))
                    for g in range(4):
                        S.pe(lambda e, g=g: e.transpose(out=c.bank[bt][:, g * 65:(g + 1) * 65], in_=oT_[0:65, g * 128:(g + 1) * 128],
                                                        identity=idf[0:65, 0:65]),
                             reads=[d_oT, c.d_ident], writes=[c.dbank[bt]], inc=(g == 3))
                    branch_evac(tt, hk, bt, 65, br, False, smm, d_smm, yc, d_yc)
                pend_fin.append([0, fin])

        n = len(its)
        for i0 in range(min(3, n)):
            front(i0)
        for i in range(n):
            if i + 3 < n:
                front(i + 3)
            back(i)
            yield

    def tile_out(tt):
        while pend_fin:
            pend_fin.pop(0)[1]()
        yc = ycs[tt % 2]
        tsl = slice(tt * 128, (tt + 1) * 128)
        S.act(lambda e: e.activation(out=ycb, in_=yc, func=AF.Copy), reads=list(d_ycs[tt % 2]), writes=[d_ycb])
        bk = next_bank(c, (6, 8))
        pt = c.bank[bk][:].bitcast(BF16)
        for j in range(4):
            S.pe(lambda e, j=j: e.transpose(out=pt[:, j * 128:(j + 1) * 128], in_=ycb[:, j * 128:(j + 1) * 128], identity=idb[:]),
                 reads=[d_ycb, c.d_ident], writes=[c.dbank[bk]], inc=(j == 3))
        S.dve(lambda e: e.tensor_copy(out=XT[:, 4:8, tsl], in_=pt[:, 0:512].rearrange("p (k t) -> p k t", k=4)),
              reads=[c.dbank[bk]], writes=[c.dXT[tt]])

    units = [(tt, hk) for tt in range(NT) for hk in range(2)]
    for _ in cmp_gen(units[0][0], units[0][1], 0):
        pass
    for i, (tt, hk) in enumerate(units):
        gm = main_gen(tt, hk, i)
        gc2 = cmp_gen(units[i + 1][0], units[i + 1][1], i + 1) if i + 1 < len(units) else None
        while gm is not None or gc2 is not None:
            if gm is not None:
                try:
                    next(gm)
                except StopIteration:
                    gm = None
            if gc2 is not None:
                try:
                    next(gc2)
                except StopIteration:
                    gc2 = None
        if hk == 1:
            tile_out(tt)


def phase_O(c, l):
    S, I, X, XT = c.S, c.I, c.X, c.XT
    S.barrier()
    A = c.arena
    A.reset()
    fb = c.ffn_bufs = Ctx()
    ln_bufs(c, fb)
    wo = A.alloc((KC, D_MODEL), BF16)
    d_wo = Dep()
    S.dma("sp", fb.G, I["ln2_g"][l:l + 1, :].broadcast_to([128, D_MODEL]), writes=[fb.d_GB])
    S.dma("sp", fb.B, I["ln2_b"][l:l + 1, :].broadcast_to([128, D_MODEL]), writes=[fb.d_GB])
    S.dma("pool", wo, I["w_out"][l].rearrange("(k p) m -> p k m", p=128), writes=[d_wo])
    for tt in range(NT):
        for half in range(2):
            bk = 4 + ((tt * 2 + half) % 3)
            mm_group(c, bk, c.bank[bk][:],
                     [(XT[:, k, tt * 128:(tt + 1) * 128], wo[:, k, half * 512:(half + 1) * 512]) for k in range(KC)],
                     reads=[d_wo, c.dXT[tt]])
            residual_half(c, tt, half, bk)
        flush_transposes(c)
        layer_norm_tile(c, tt)
    flush_transposes(c)


_NC_CACHE = {}


def _consts():
    f = np.float32
    i = np.arange(128)
    same = (i[:, None] // 64) == (i[None, :] // 64)
    cst = {"c_ident": np.eye(128, dtype=f)}
    cst["c_bu"] = (same & (i[:, None] <= i[None, :])).astype(f)
    cst["c_he"] = np.broadcast_to((i[:, None] < 64), (128, 128)).astype(f)
    cst["c_ho"] = np.broadcast_to((i[:, None] >= 64), (128, 128)).astype(f)
    cst["c_m1"] = np.where(same & (i[:, None] > i[None, :]), 0.0, MNEG).astype(f)
    cst["c_m2"] = np.where(same & (i[None, :] >= i[:, None]), 0.0, MNEG).astype(f)
    n = np.arange(127)
    t = np.arange(SEQ)
    cst["c_cm"] = np.where((16 * n[:, None] + 31) <= t[None, :], 0.0, MNEG).astype(f)
    qb = t // 64
    j = np.arange(32)
    forced = (j[None, :] == 0) | (j[None, :] == qb[:, None]) | (j[None, :] == qb[:, None] - 1)
    fbv = np.where(forced, 1e3, 0.0)
    fbv = np.where(j[None, :] <= qb[:, None], fbv, NEG).astype(f)
    cst["c_fb"] = np.ascontiguousarray(fbv.reshape(NT, 128, 32).transpose(1, 0, 2).reshape(128, NT * 32))
    kb = np.arange(NT)
    exv = (j[:, None, None] == (2 * kb[None, :, None] + i[None, None, :] // 64)).astype(f)
    cst["c_ex"] = np.ascontiguousarray(exv.reshape(32, NT * 128))
    cst["c_caus"] = np.where(i[:, None] <= i[None, :], 0.0, MNEG).astype(f)
    cst["c_win"] = np.where(i[:, None] > i[None, :], 0.0, MNEG).astype(f)
    cmp_tok = (t[None, :] >= 16 * n[:, None]) & (t[None, :] < 16 * n[:, None] + 32)
    sel_tok = (t[None, :] // 64) == j[:, None]
    cst["c_ov"] = (cmp_tok.astype(f) @ sel_tok.astype(f).T / 32.0).astype(f)
    rk = np.zeros((128, 6, 256), f)
    for k in range(6):
        sz = 1 << k
        mk = ((i[:, None] // (2 * sz)) == (i[None, :] // (2 * sz))) & ((i[:, None] % (2 * sz)) >= sz) & ((i[None, :] % (2 * sz)) < sz)
        rk[:, k, 0:128] = mk
        rk[:, k, 128:256] = mk.T
    cst["c_rk"] = np.ascontiguousarray(rk.reshape(128, 6 * 256))
    return cst


def kernel(**inputs):
    if "nc" not in _NC_CACHE:
        _NC_CACHE["nc"] = build()
    nc = _NC_CACHE["nc"]
    consts = _consts()
    x = np.ascontiguousarray(inputs["x"], dtype=np.float32)
    in_maps = []
    for b in range(8):
        m = {k: np.ascontiguousarray(v) for k, v in inputs.items() if k != "x"}
        m["x"] = x[b]
        m.update(consts)
        in_maps.append(m)
    res = run_bass_kernel_spmd(nc, in_maps, core_ids=list(range(8)))
    return np.stack([r["out"] for r in res.results], axis=0).astype(np.float32)
```

```python
import numpy as np
from contextlib import ExitStack
import concourse.bass as bass
import concourse.mybir as mybir
from concourse.bass_utils import run_bass_kernel_spmd

F32 = mybir.dt.float32
BF16 = mybir.dt.bfloat16
AF = mybir.ActivationFunctionType
ALU = mybir.AluOpType
AX = mybir.AxisListType

D_MODEL = 1024
SEQ = 2048
DEPTH = 2
D_FF = 2816
IN_WIDTH = 3104
NT = SEQ // 128
KC = D_MODEL // 128
FC = D_FF // 128
ALPHA = float((2 * DEPTH) ** 0.25)
LN_EPS = 1e-5
NORM_EPS = 1e-6
NEG = -1e30

O_CB, O_CC, O_CH = 0, 256, 512
O_GQKV = 768
O_GZ = O_GQKV + 768
O_GA = O_GZ + 256
O_GB = O_GA + 4
O_NQ = O_GB + 4
O_NKV = O_NQ + 512
O_NG = O_NKV + 768


class Dep:
    __slots__ = ("w", "r")

    def __init__(self):
        self.w = None
        self.r = {}


class Sched:
    ENG = ("pe", "dve", "act", "pool", "sp")

    def __init__(self, nc, es, n_dma_sems=24):
        self.nc = nc
        self.q = {e: [] for e in self.ENG}
        self.cnt = {e: 0 for e in self.ENG}
        self.waited = {e: {} for e in self.ENG}
        self.sems = {}
        for e in ("pe", "dve", "act", "pool"):
            self.sems[e] = es.enter_context(nc.semaphore("s_" + e))
        self.dma_keys = []
        self.dma_pool = {"sp": [], "pool": []}
        for qn, n in (("sp", n_dma_sems), ("pool", 12)):
            for i in range(n):
                k = "d%s%d" % (qn, i)
                self.sems[k] = es.enter_context(nc.semaphore("s_" + k))
                self.dma_keys.append(k)
                self.dma_pool[qn].append(k)
        self.dma_val = {k: 0 for k in self.dma_keys}
        self.dma_i = {"sp": 0, "pool": 0}
        self.out_tokens = []

    def _emit_waits(self, e, waits):
        for k, v in waits.items():
            if self.waited[e].get(k, 0) < v:
                self.waited[e][k] = v
                sem = self.sems[k]
                self.q[e].append(lambda eng, sem=sem, v=v: eng.wait_ge(sem, v))

    @staticmethod
    def _add(waits, tok):
        k, v = tok
        if waits.get(k, 0) < v:
            waits[k] = v

    def _collect(self, e, reads, writes):
        waits = {}
        for d in reads:
            if d.w is not None:
                self._add(waits, d.w)
        for d in writes:
            if d.w is not None and (d.w[0] != e or e != "pe"):
                self._add(waits, d.w)
            for k, v in d.r.items():
                if k != e or e != "pe":
                    self._add(waits, (k, v))
        return waits

    def op(self, e, fn, reads=(), writes=(), inc=True):
        self._emit_waits(e, self._collect(e, reads, writes))
        if inc:
            self.cnt[e] += 1
            idx = self.cnt[e]
            sem = self.sems[e]
            self.q[e].append(lambda eng, fn=fn, sem=sem: fn(eng).then_inc(sem, 1))
        else:
            idx = self.cnt[e] + 1
            self.q[e].append(lambda eng, fn=fn: fn(eng))
        for d in reads:
            if d.r.get(e, 0) < idx:
                d.r[e] = idx
        for d in writes:
            d.w = (e, idx)
            d.r = {}
        return (e, idx)

    def pe(self, fn, reads=(), writes=(), inc=True):
        return self.op("pe", fn, reads, writes, inc)

    def dve(self, fn, reads=(), writes=()):
        return self.op("dve", fn, reads, writes)

    def act(self, fn, reads=(), writes=()):
        return self.op("act", fn, reads, writes)

    def pool(self, fn, reads=(), writes=()):
        return self.op("pool", fn, reads, writes)

    def dma(self, qe, out, in_, reads=(), writes=(), is_output=False, **kw):
        waits = self._collect("__dma__", reads, writes)
        pl = self.dma_pool[qe]
        k = pl[self.dma_i[qe] % len(pl)]
        self.dma_i[qe] += 1
        if self.dma_val[k] > 0:
            self._add(waits, (k, self.dma_val[k]))
        self._emit_waits(qe, waits)
        self.dma_val[k] += 16
        v = self.dma_val[k]
        sem = self.sems[k]
        self.q[qe].append(lambda eng, out=out, in_=in_, sem=sem, kw=kw:
                          eng.dma_start(out=out, in_=in_, **kw).then_inc(sem, 16))
        for d in reads:
            if d.r.get(k, 0) < v:
                d.r[k] = v
        for d in writes:
            d.w = (k, v)
            d.r = {}
        if is_output:
            self.out_tokens.append((k, v))
        return (k, v)

    def barrier(self):
        allw = {}
        for e in ("pe", "dve", "act", "pool"):
            if self.cnt[e] > 0:
                allw[e] = self.cnt[e]
        for k in self.dma_keys:
            if self.dma_val[k] > 0:
                allw[k] = self.dma_val[k]
        for e in self.ENG:
            self._emit_waits(e, dict(allw))

    def finish(self):
        waits = {}
        for t in self.out_tokens:
            self._add(waits, t)
        self._emit_waits("sp", waits)

    def run(self, block):
        nc = self.nc
        q = self.q

        @block.tensor
        def _(eng):
            for f in q["pe"]:
                f(eng)

        @block.vector
        def _(eng):
            for f in q["dve"]:
                f(eng)

        @block.scalar
        def _(eng):
            for f in q["act"]:
                f(eng)

        @block.gpsimd
        def _(eng):
            for f in q["pool"]:
                f(eng)

        @block.sync
        def _(eng):
            for f in q["sp"]:
                f(eng)


class Ctx:
    pass


class Arena:
    def __init__(self, t, nelem):
        self.t = t
        self.n = nelem
        self.off = 0
        self.log = []

    def reset(self):
        self.off = 0
        self.log = []

    def alloc(self, free_shape, dt, parts=128):
        n = 1
        for v in free_shape:
            n *= v
        ne = n * 2 if dt == F32 else n
        self.off = (self.off + 1) // 2 * 2
        assert self.off + ne <= self.n, ("arena overflow", self.off, ne, self.n)
        v = self.t[0:parts, self.off:self.off + ne]
        self.log.append((self.off, ne))
        self.off += ne
        if dt == F32:
            v = v.bitcast(F32)
        if len(free_shape) == 2:
            v = v.rearrange("p (a b) -> p a b", a=free_shape[0])
        elif len(free_shape) == 3:
            v = v.rearrange("p (a b c) -> p a b c", a=free_shape[0], b=free_shape[1])
        elif len(free_shape) == 4:
            v = v.rearrange("p (a b c d) -> p a b c d", a=free_shape[0], b=free_shape[1], c=free_shape[2])
        return v


def build(stage="full", dbg=(), skip_ffn=False):
    nc = bass.Bass("TRN2", target_bir_lowering=False)
    es = ExitStack()
    c = Ctx()
    c.nc = nc
    c.es = es
    c.dbg = {}
    c.no_arena_dump = ("noarena" in dbg)

    def dram_in(name, shape, dt=F32):
        return nc.dram_tensor(name, list(shape), dt, kind="ExternalInput").ap()

    L = DEPTH
    I = c.I = {}
    I["x"] = dram_in("x", (SEQ, D_MODEL))
    for nm, shp in [
        ("ffn1_w_gate", (L, D_MODEL, D_FF)), ("ffn1_w_up", (L, D_MODEL, D_FF)),
        ("ffn1_w_down", (L, D_FF, D_MODEL)), ("ln1_g", (L, D_MODEL)), ("ln1_b", (L, D_MODEL)),
        ("w_in", (L, D_MODEL, IN_WIDTH)), ("conv_w", (L, 256, 3)), ("gdn_conv_w", (L, 768, 4)),
        ("gdn_a_log", (L, 4)), ("gdn_dt_bias", (L, 4)), ("gdn_norm_w", (L, 64)),
        ("cmp_pe_k", (L, 32, 64)), ("cmp_pe_v", (L, 32, 64)),
        ("cmp_k_w1", (L, 2048, 128)), ("cmp_k_w2", (L, 128, 64)),
        ("cmp_v_w1", (L, 2048, 128)), ("cmp_v_w2", (L, 128, 64)),
        ("w_out", (L, D_MODEL, D_MODEL)), ("ln2_g", (L, D_MODEL)), ("ln2_b", (L, D_MODEL)),
        ("ffn2_w_gate", (L, D_MODEL, D_FF)), ("ffn2_w_up", (L, D_MODEL, D_FF)),
        ("ffn2_w_down", (L, D_FF, D_MODEL)), ("ln3_g", (L, D_MODEL)), ("ln3_b", (L, D_MODEL)),
    ]:
        I[nm] = dram_in(nm, shp)
    I["ident"] = dram_in("c_ident", (128, 128))
    for nm, shp in [("bu", (128, 128)), ("he", (128, 128)), ("ho", (128, 128)), ("m1", (128, 128)), ("m2", (128, 128)),
                    ("cm", (127, NT * 128)), ("fb", (128, NT * 32)), ("ex", (32, NT * 128)), ("caus", (128, 128)),
                    ("win", (128, 128)), ("ov", (127, 32)), ("rk", (128, 6 * 256))]:
        I[nm] = dram_in("c_" + nm, shp)
    out = c.out = nc.dram_tensor("out", [SEQ, D_MODEL], F32, kind="ExternalOutput").ap()

    S = c.S = Sched(nc, es)

    def sb(name, shape, dt=F32):
        return es.enter_context(nc.sbuf_tensor(name, list(shape), dt))

    def ps(name, shape, dt=F32):
        return es.enter_context(nc.psum_tensor(name, list(shape), dt))

    c.sb, c.ps = sb, ps

    X = c.X = sb("X", [128, NT, D_MODEL], F32)
    XT = c.XT = sb("XT", [128, KC, SEQ], BF16)
    c.dX = [Dep() for _ in range(NT)]
    c.dXT = [Dep() for _ in range(NT)]
    ident_f = c.ident_f = sb("ident_f", [128, 128], F32)
    ident_b = c.ident_b = sb("ident_b", [128, 128], BF16)
    c.d_ident = Dep()
    S.dma("sp", ident_f[:], I["ident"][:, :], writes=[c.d_ident])
    S.dve(lambda e: e.tensor_copy(out=ident_b[:], in_=ident_f[:]), reads=[c.d_ident], writes=[c.d_ident])

    c.stage = stage
    skind = "ExternalOutput" if stage[0] in "PCGN" else "Internal"
    c.HT = nc.dram_tensor("scr_ht", [14 * 128, SEQ], F32, kind=skind).ap()
    c.HB = nc.dram_tensor("scr_hb", [6 * 128, SEQ], BF16, kind=skind).ap()
    c.HK = nc.dram_tensor("scr_hk", [SEQ, 544], F32, kind=skind).ap()
    c.dHT = [[Dep() for _ in range(4)] for _ in range(14)]
    c.dHB = [[Dep() for _ in range(4)] for _ in range(6)]
    c.dHK = [Dep() for _ in range(NT)]
    c.WBs = [nc.dram_tensor("scr_wb%d" % i, [FC // 2, 128, 2 * KC * 256], BF16, kind="Internal").ap() for i in range(2)]
    c.dWBs = [[Dep() for _ in range(FC // 2)] for i in range(2)]
    c.WDs = [nc.dram_tensor("scr_wd%d" % i, [128, FC * D_MODEL], BF16, kind="Internal").ap() for i in range(2)]
    c.dWDs = [[Dep() for _ in range(4)] for i in range(2)]
    c.ffn_i = 0
    c.preconv = set()
    c.QC = nc.dram_tensor("scr_qc", [6 * 128, SEQ], F32, kind="Internal").ap()
    c.dQC = [Dep() for _ in range(6)]
    c.ab_sb = sb("ab_sb", [128, NT, 8], F32)
    c.gt_sb = sb("gt_sb", [128, NT, 24], F32)
    c.d_ab, c.d_gt = Dep(), Dep()

    c.bank = [ps("bank%d" % i, [128, 512], F32) for i in range(8)]
    c.dbank = [Dep() for _ in range(8)]

    xv = I["x"].rearrange("(t p) d -> p t d", p=128)
    for q4 in range(4):
        S.dma("sp", X[:, q4 * 4:(q4 + 1) * 4, :], xv[:, q4 * 4:(q4 + 1) * 4, :],
              writes=[c.dX[t] for t in range(q4 * 4, q4 * 4 + 4)])
    ARENA_N = 53248
    c.arena = Arena(sb("arena", [128, ARENA_N], BF16), ARENA_N)
    xb_tmp = c.xb_tmp = sb("xb_tmp", [128, D_MODEL], BF16)
    c.d_xb = Dep()
    for tt in range(NT):
        emit_transposes(c, tt, src_f32=True)

    for l in range(DEPTH):
        if not skip_ffn:
            ffn(c, l, 1)
        if stage == "ffn1" and l == 0:
            break
        mixer(c, l)
        if (stage[0] in "PCGN" or stage == "mix") and l == 0:
            break
        ffn(c, l, 2)
        if stage == "layer0":
            break

    if stage[0] in "CGN":
        dbg_xt = nc.dram_tensor("dbg_xt", [128, KC, SEQ], BF16, kind="ExternalOutput").ap()
        S.dma("sp", dbg_xt, XT[:], reads=list(c.dXT), is_output=True)
    ov = out.rearrange("(t p) d -> p t d", p=128)
    for q4 in range(4):
        S.dma("sp", ov[:, q4 * 4:(q4 + 1) * 4, :], X[:, q4 * 4:(q4 + 1) * 4, :],
              reads=[c.dX[t] for t in range(q4 * 4, q4 * 4 + 4)], is_output=True)
    S.finish()
    with nc.Block() as block:
        S.run(block)
    es.close()
    return nc


def emit_transposes(c, tt, src_f32=True):
    S = c.S
    X, XT = c.X, c.XT
    xb = c.xb_tmp
    S.act(lambda e: e.activation(out=xb[:], in_=X[:, tt, :], func=AF.Copy),
          reads=[c.dX[tt]], writes=[c.d_xb])
    bk = 7
    pt = c.bank[bk][:].bitcast(BF16)
    for k in range(KC):
        S.pe(lambda e, k=k: e.transpose(out=pt[:, k * 128:(k + 1) * 128], in_=xb[:, k * 128:(k + 1) * 128],
                                        identity=c.ident_b[:]),
             reads=[c.d_xb, c.d_ident], writes=[c.dbank[bk]], inc=(k == KC - 1))
    S.dve(lambda e: e.tensor_copy(out=XT[:, :, tt * 128:(tt + 1) * 128],
                                  in_=pt.rearrange("p (k t) -> p k t", k=KC)),
          reads=[c.dbank[bk]], writes=[c.dXT[tt]])


def ffn(c, l, which):
    S, nc, I = c.S, c.nc, c.I
    X, XT = c.X, c.XT
    pre = "ffn%d_" % which
    wg_d, wu_d, wd_d = I[pre + "w_gate"], I[pre + "w_up"], I[pre + "w_down"]
    lg_d, lb_d = (I["ln1_g"], I["ln1_b"]) if which == 1 else (I["ln3_g"], I["ln3_b"])
    S.barrier()
    A = c.arena
    A.reset()
    fb = c.ffn_bufs = Ctx()
    ln_bufs(c, fb)
    fb.wd = A.alloc((FC, D_MODEL), BF16)
    fb.d_wd = [Dep() for _ in range(4)]
    fb.hT = A.alloc((FC, 512), BF16)
    fb.d_hT = [Dep() for _ in range(FC)]
    fb.NS = 2
    fb.wgu = [A.alloc((2, KC, 256), BF16) for i in range(fb.NS)]
    fb.d_wgu = [Dep() for _ in range(fb.NS)]
    fb.sg = [A.alloc((512,), F32) for i in range(2)]
    fb.d_sg = [Dep() for _ in range(2)]
    fb.slot_i = 0
    fb.ybank_i = 0
    S.dma("sp", fb.G[:], lg_d[l:l + 1, :].broadcast_to([128, D_MODEL]), writes=[fb.d_GB])
    S.dma("sp", fb.B[:], lb_d[l:l + 1, :].broadcast_to([128, D_MODEL]), writes=[fb.d_GB])
    wgv = wg_d[l].rearrange("(k p) f -> p k f", p=128)
    wuv = wu_d[l].rearrange("(k p) f -> p k f", p=128)
    wdv = wd_d[l].rearrange("(f p) m -> p f m", p=128)
    wd_loaded = False
    fidx = l * 2 + (which - 1)
    c.WB, c.dWB = c.WBs[fidx % 2], c.dWBs[fidx % 2]
    pre = fidx in c.preconv
    if which == 2 and l + 1 < DEPTH:
        preconvert_ffn(c, l + 1, 1)
    for tb in range(4):
        tsl = slice(tb * 512, (tb + 1) * 512)
        for fg in range(FC // 2):
            s = fb.slot_i % fb.NS
            fb.slot_i += 1
            wt = fb.wgu[s]
            if tb == 0 and not pre:
                S.dma("pool", wt[:, 0, :, :], wgv[:, :, fg * 256:(fg + 1) * 256], writes=[fb.d_wgu[s]])
                S.dma("pool", wt[:, 1, :, :], wuv[:, :, fg * 256:(fg + 1) * 256], writes=[fb.d_wgu[s]])
                S.dma("sp", c.WB[fg], wt[:].rearrange("p a k f -> p (a k f)"), reads=[fb.d_wgu[s]], writes=[c.dWB[fg]])
            else:
                S.dma("sp", wt[:].rearrange("p a k f -> p (a k f)"), c.WB[fg], reads=[c.dWB[fg]], writes=[fb.d_wgu[s]])
            if tb == 0 and not wd_loaded and fg == 1:
                for qd in range(4):
                    fsl = slice(qd * 6, min(FC, (qd + 1) * 6))
                    if pre:
                        S.dma("sp", fb.wd[:, fsl, :], c.WDs[fidx % 2].rearrange("p (f m) -> p f m", f=FC)[:, fsl, :],
                              reads=[c.dWDs[fidx % 2][qd]], writes=[fb.d_wd[qd]])
                    else:
                        S.dma("pool", fb.wd[:, fsl, :], wdv[:, fsl, :], writes=[fb.d_wd[qd]])
                wd_loaded = True
            for fc2 in range(2):
                f = fg * 2 + fc2
                gb, ub = (0, 1) if f % 2 == 0 else (2, 3)
                for (bk, wi) in ((gb, 0), (ub, 1)):
                    for k in range(KC):
                        S.pe(lambda e, bk=bk, wi=wi, k=k, wt=wt, fc2=fc2, tsl=tsl:
                             e.matmul(c.bank[bk][:], lhsT=wt[:, wi, k, fc2 * 128:(fc2 + 1) * 128],
                                      rhs=XT[:, k, tsl], start=(k == 0), stop=(k == KC - 1)),
                             reads=[fb.d_wgu[s]] + [c.dXT[tb * 4 + i] for i in range(4)],
                             writes=[c.dbank[bk]], inc=(k == KC - 1))
                sg = fb.sg[f % 2]
                S.act(lambda e, sg=sg, gb=gb: e.activation(out=sg[:], in_=c.bank[gb][:], func=AF.Silu),
                      reads=[c.dbank[gb]], writes=[fb.d_sg[f % 2]])
                S.dve(lambda e, sg=sg, ub=ub, f=f: e.scalar_tensor_tensor(
                    out=fb.hT[:, f, :], in0=c.bank[ub][:], scalar=0.5, in1=sg[:], op0=ALU.mult, op1=ALU.mult),
                    reads=[c.dbank[ub], fb.d_sg[f % 2]], writes=[fb.d_hT[f]])
        for ti in range(4):
            tt = tb * 4 + ti
            for half in range(2):
                bk = 4 + (fb.ybank_i % 3)
                fb.ybank_i += 1
                for f in range(FC):
                    S.pe(lambda e, bk=bk, f=f, ti=ti, half=half:
                         e.matmul(c.bank[bk][:], lhsT=fb.hT[:, f, ti * 128:(ti + 1) * 128],
                                  rhs=fb.wd[:, f, half * 512:(half + 1) * 512], start=(f == 0), stop=(f == FC - 1)),
                         reads=[fb.d_hT[f], fb.d_wd[min(3, f // 6)]], writes=[c.dbank[bk]], inc=(f == FC - 1))
                residual_half(c, tt, half, bk)
            flush_transposes(c)
            layer_norm_tile(c, tt)
    flush_transposes(c)


def preconvert_ffn(c, l, which):
    S, I = c.S, c.I
    pre = "ffn%d_" % which
    fidx = l * 2 + (which - 1)
    WB, dWB = c.WBs[fidx % 2], c.dWBs[fidx % 2]
    wgv = I[pre + "w_gate"][l].rearrange("(k p) f -> p k f", p=128)
    wuv = I[pre + "w_up"][l].rearrange("(k p) f -> p k f", p=128)
    wdv = I[pre + "w_down"][l].rearrange("(f p) m -> p f m", p=128)
    for fg in range(FC // 2):
        dst = WB[fg].rearrange("p (a k f) -> p a k f", a=2, k=KC)
        S.dma("pool", dst[:, 0, :, :], wgv[:, :, fg * 256:(fg + 1) * 256], writes=[dWB[fg]])
        S.dma("pool", dst[:, 1, :, :], wuv[:, :, fg * 256:(fg + 1) * 256], writes=[dWB[fg]])
    wdd = c.WDs[fidx % 2].rearrange("p (f m) -> p f m", f=FC)
    for qd in range(4):
        fsl = slice(qd * 6, min(FC, (qd + 1) * 6))
        S.dma("pool", wdd[:, fsl, :], wdv[:, fsl, :], writes=[c.dWDs[fidx % 2][qd]])
    c.preconv.add(fidx)


def ln_bufs(c, fb):
    A = c.arena
    fb.G = A.alloc((D_MODEL,), F32)
    fb.B = A.alloc((D_MODEL,), F32)
    fb.d_GB = Dep()
    fb.rs = [A.alloc((D_MODEL,), F32) for _ in range(2)]
    fb.d_rs = [Dep(), Dep()]
    fb.sts = [A.alloc((20,), F32) for _ in range(2)]
    fb.d_sts = [Dep(), Dep()]
    fb.pend = []


def layer_norm_tile(c, tt):
    S = c.S
    fb = c.ffn_bufs
    X = c.X
    r, st = fb.rs[tt % 2], fb.sts[tt % 2]
    d_r, d_st = fb.d_rs[tt % 2], fb.d_sts[tt % 2]
    S.dve(lambda e: e.bn_aggr(out=st[:, 12:14], in_=st[:, 0:12]), reads=[d_st], writes=[d_st])
    S.act(lambda e: e.activation(out=st[:, 14:15], in_=st[:, 13:14], func=AF.Sqrt, bias=LN_EPS), reads=[d_st], writes=[d_st])
    S.dve(lambda e: e.reciprocal(out=st[:, 15:16], in_=st[:, 14:15]), reads=[d_st], writes=[d_st])
    S.dve(lambda e: e.scalar_tensor_tensor(out=st[:, 16:17], in0=st[:, 12:13], scalar=-1.0, in1=st[:, 15:16],
                                           op0=ALU.mult, op1=ALU.mult), reads=[d_st], writes=[d_st])
    S.act(lambda e: e.activation(out=r, in_=r, func=AF.Identity, scale=st[:, 15:16], bias=st[:, 16:17]),
          reads=[d_r, d_st], writes=[d_r])
    S.dve(lambda e: e.tensor_tensor(out=r, in0=r, in1=fb.G, op=ALU.mult), reads=[d_r, fb.d_GB], writes=[d_r])
    S.dve(lambda e: e.tensor_tensor(out=X[:, tt, :], in0=r, in1=fb.B, op=ALU.add), reads=[d_r, fb.d_GB], writes=[c.dX[tt]])
    fb.pend.append(tt)


def residual_half(c, tt, half, bk):
    S = c.S
    fb = c.ffn_bufs
    r, st = fb.rs[tt % 2], fb.sts[tt % 2]
    d_r, d_st = fb.d_rs[tt % 2], fb.d_sts[tt % 2]
    hs = slice(half * 512, (half + 1) * 512)
    S.dve(lambda e: e.scalar_tensor_tensor(out=r[:, hs], in0=c.X[:, tt, hs], scalar=ALPHA, in1=c.bank[bk][:],
                                           op0=ALU.mult, op1=ALU.add), reads=[c.dX[tt], c.dbank[bk]], writes=[d_r])
    S.dve(lambda e: e.bn_stats(out=st[:, half * 6:(half + 1) * 6], in_=r[:, hs]), reads=[d_r], writes=[d_st])


def flush_transposes(c, keep=0):
    fb = c.ffn_bufs
    while len(fb.pend) > keep:
        emit_transposes(c, fb.pend.pop(0))


def next_bank(c, pool=None):
    if pool is None:
        c.bank_i = (getattr(c, "bank_i", -1) + 1) % 8
        return c.bank_i
    key = "bank_i_%d_%d" % (pool[0], pool[1])
    i = getattr(c, key, -1) + 1
    setattr(c, key, i)
    return pool[0] + i % (pool[1] - pool[0])


def mm_group(c, bk, out_ap, pairs, reads, start=True, stop=True):
    S = c.S
    n = len(pairs)
    for i, (lt, rh) in enumerate(pairs):
        S.pe(lambda e, lt=lt, rh=rh, i=i: e.matmul(out_ap, lhsT=lt, rhs=rh, start=(start and i == 0),
                                                  stop=(stop and i == n - 1)),
             reads=reads, writes=[c.dbank[bk]], inc=(i == n - 1))


def mixer(c, l):
    phase_P(c, l)
    if c.stage == "P":
        return
    phase_C(c, l)
    if c.stage == "C":
        return
    phase_G(c, l)
    if c.stage[0] == "G":
        return
    phase_N(c, l)
    if c.stage[0] == "N":
        return
    phase_O(c, l)


def phase_P(c, l):
    S, I, XT = c.S, c.I, c.XT
    S.barrier()
    A = c.arena
    A.reset()
    wfm = [A.alloc((KC, 512), BF16) for _ in range(2)]
    d_wfm = [Dep(), Dep()]
    stgf = [A.alloc((4, 512), F32) for _ in range(2)]
    stgb = [A.alloc((4, 512), BF16) for _ in range(2)]
    d_stg = [Dep(), Dep()]
    wtm = A.alloc((KC, 544), BF16)
    d_wtm = Dep()
    stt = [A.alloc((544,), F32) for _ in range(2)]
    d_stt = [Dep(), Dep()]
    wv = I["w_in"][l].rearrange("(k p) n -> p k n", p=128)
    HTv = c.HT.rearrange("(c p) t -> p c t", p=128)
    HBv = c.HB.rearrange("(c p) t -> p c t", p=128)
    groups = [(0, 4, "T", 0), (512, 4, "T", 4), (1024, 4, "T", 8), (2312, 2, "T", 12),
              (1800, 4, "B", 0), (2568, 1, "B", 4), (2824, 1, "B", 5)]
    gi = 0
    si = 0
    ev = 0
    for (col0, nch, dest, ch0) in groups:
        w = wfm[gi % 2]
        dw = d_wfm[gi % 2]
        gi += 1
        S.dma("pool", w[:, :, 0:nch * 128], wv[:, :, col0:col0 + nch * 128], writes=[dw])
        for tb in range(4):
            stg = (stgf if dest == "T" else stgb)[si % 2]
            dstg = d_stg[si % 2]
            si += 1
            for ci in range(nch):
                bk = next_bank(c)
                mm_group(c, bk, c.bank[bk][:],
                         [(w[:, k, ci * 128:(ci + 1) * 128], XT[:, k, tb * 512:(tb + 1) * 512]) for k in range(KC)],
                         reads=[dw] + [c.dXT[tb * 4 + i] for i in range(4)])
                if ev % 2 == 0:
                    S.act(lambda e, stg=stg, ci=ci, bk=bk: e.activation(out=stg[:, ci, :], in_=c.bank[bk][:], func=AF.Copy),
                          reads=[c.dbank[bk]], writes=[dstg])
                else:
                    S.dve(lambda e, stg=stg, ci=ci, bk=bk: e.tensor_copy(out=stg[:, ci, :], in_=c.bank[bk][:]),
                          reads=[c.dbank[bk]], writes=[dstg])
                ev += 1
            dv, dd = (HTv, c.dHT) if dest == "T" else (HBv, c.dHB)
            S.dma("sp", dv[:, ch0:ch0 + nch, tb * 512:(tb + 1) * 512], stg[:, 0:nch, :],
                  reads=[dstg], writes=[dd[ch0 + ci][tb] for ci in range(nch)])
    for (c0, n, o) in [(1536, 264, 0), (2696, 128, 264), (2952, 128, 392), (3080, 24, 520)]:
        S.dma("pool", wtm[:, :, o:o + n], wv[:, :, c0:c0 + n], writes=[d_wtm])
    for tt in range(NT):
        st = stt[tt % 2]
        dst = d_stt[tt % 2]
        for (o, n) in ((0, 264), (264, 280)):
            bk = next_bank(c)
            mm_group(c, bk, c.bank[bk][:, 0:n],
                     [(XT[:, k, tt * 128:(tt + 1) * 128], wtm[:, k, o:o + n]) for k in range(KC)],
                     reads=[d_wtm, c.dXT[tt]])
            if o == 0:
                S.act(lambda e, st=st, bk=bk, o=o, n=n: e.activation(out=st[:, o:o + n], in_=c.bank[bk][:, 0:n], func=AF.Copy),
                      reads=[c.dbank[bk]], writes=[dst])
            else:
                S.dve(lambda e, st=st, bk=bk, o=o, n=n: e.tensor_copy(out=st[:, o:o + n], in_=c.bank[bk][:, 0:n]),
                      reads=[c.dbank[bk]], writes=[dst])
        S.pool(lambda e, st=st, tt=tt: e.tensor_copy(out=c.ab_sb[:, tt, :], in_=st[:, 256:264]),
               reads=[dst], writes=[c.d_ab])
        S.pool(lambda e, st=st, tt=tt: e.tensor_copy(out=c.gt_sb[:, tt, :], in_=st[:, 520:544]),
               reads=[dst], writes=[c.d_gt])
        S.dma("sp", c.HK[tt * 128:(tt + 1) * 128, :], st[:], reads=[dst], writes=[c.dHK[tt]])


def phase_C(c, l):
    S, I, XT = c.S, c.I, c.XT
    S.barrier()
    A = c.arena
    A.reset()
    cw = A.alloc((2, 3), F32)
    d_cw = Dep()
    S.dma("sp", cw, I["conv_w"][l].rearrange("(c p) k -> p c k", p=128), writes=[d_cw])
    cbt = A.alloc((SEQ,), F32)
    cct = A.alloc((SEQ,), F32)
    uu = A.alloc((SEQ + 2,), F32)
    acc = A.alloc((SEQ,), F32)
    d_cb, d_cc, d_uu, d_acc = Dep(), Dep(), Dep(), Dep()
    HTv = c.HT.rearrange("(c p) t -> p c t", p=128)
    for ch in range(2):
        S.dma("sp", cbt, HTv[:, 0 + ch, :], reads=c.dHT[0 + ch], writes=[d_cb])
        S.dma("sp", cct, HTv[:, 2 + ch, :], reads=c.dHT[2 + ch], writes=[d_cc])
        S.pool(lambda e: e.memset(uu[:, 0:2], 0.0), writes=[d_uu])
        S.dma("sp", uu[:, 2:SEQ + 2], HTv[:, 4 + ch, :], reads=c.dHT[4 + ch], writes=[d_uu])
        S.dve(lambda e: e.tensor_tensor(out=uu[:, 2:SEQ + 2], in0=uu[:, 2:SEQ + 2], in1=cct, op=ALU.mult),
              reads=[d_uu, d_cc], writes=[d_uu])
        S.dve(lambda e, ch=ch: e.tensor_scalar(out=acc, in0=uu[:, 0:SEQ], scalar1=cw[:, ch, 0:1], scalar2=None,
                                               op0=ALU.mult), reads=[d_uu, d_cw], writes=[d_acc])
        for j in (1, 2):
            S.dve(lambda e, ch=ch, j=j: e.scalar_tensor_tensor(out=acc, in0=uu[:, j:SEQ + j], scalar=cw[:, ch, j:j + 1],
                                                               in1=acc, op0=ALU.mult, op1=ALU.add),
                  reads=[d_uu, d_cw, d_acc], writes=[d_acc])
        S.dve(lambda e, ch=ch: e.tensor_tensor(out=XT[:, ch, :], in0=cbt, in1=acc, op=ALU.mult),
              reads=[d_cb, d_acc], writes=list(c.dXT))


def phase_G(c, l):
    S, I, XT = c.S, c.I, c.XT
    idf = c.ident_f
    S.barrier()
    A = c.arena
    A.reset()
    HTv = c.HT.rearrange("(c p) t -> p c t", p=128)
    gw = A.alloc((6, 4), F32)
    d_gw = Dep()
    S.dma("sp", gw, I["gdn_conv_w"][l].rearrange("(c p) k -> p c k", p=128), writes=[d_gw])
    uu = A.alloc((SEQ + 3,), F32)
    acc = A.alloc((SEQ,), F32)
    cvo = A.alloc((SEQ,), F32)
    d_uu, d_acc, d_cvo = Dep(), Dep(), Dep()
    QCv = c.QC.rearrange("(c p) t -> p c t", p=128)
    for ch in range(6):
        S.pool(lambda e: e.memset(uu[:, 0:3], 0.0), writes=[d_uu])
        S.dma("sp", uu[:, 3:SEQ + 3], HTv[:, 6 + ch, :], reads=c.dHT[6 + ch], writes=[d_uu])
        S.dve(lambda e, ch=ch: e.tensor_scalar(out=acc, in0=uu[:, 0:SEQ], scalar1=gw[:, ch, 0:1], scalar2=None,
                                               op0=ALU.mult), reads=[d_uu, d_gw], writes=[d_acc])
        for j in (1, 2, 3):
            S.dve(lambda e, ch=ch, j=j: e.scalar_tensor_tensor(out=acc, in0=uu[:, j:SEQ + j], scalar=gw[:, ch, j:j + 1],
                                                               in1=acc, op0=ALU.mult, op1=ALU.add),
                  reads=[d_uu, d_gw, d_acc], writes=[d_acc])
        S.act(lambda e: e.activation(out=cvo, in_=acc, func=AF.Silu), reads=[d_acc], writes=[d_cvo])
        S.dma("sp", QCv[:, ch, :], cvo, reads=[d_cvo], writes=[c.dQC[ch]])
    if c.stage == "G1":
        return
    S.barrier()
    A.reset()
    NCOL = NT * 4
    cst = A.alloc((6, 128), F32)
    d_cst = Dep()
    for i_, nm in enumerate(["bu", "he", "ho", "m1", "m2"]):
        S.dma("sp", cst[:, i_, :], I[nm][:, :], writes=[d_cst])
    RK = A.alloc((6, 256), F32)
    S.dma("sp", RK, I["rk"].rearrange("p (k n) -> p k n", k=6), writes=[d_cst])
    I4 = A.alloc((2, 128), F32)
    for h_ in range(2):
        S.dma("sp", I4[:, h_, :], I["ident"][:, :], writes=[d_cst])
    S.dve(lambda e: e.memset(cst[:, 5, :], 1.0), writes=[d_cst])
    BU, HE, HO, M1, M2, ONES = (cst[:, i_, :] for i_ in range(6))
    M12 = cst[:, 3:5, :].rearrange("p a b -> p (a b)")
    hp = A.alloc((3, 4), F32)
    d_hp = Dep()
    S.dma("sp", hp[:, 0, :], I["gdn_dt_bias"][l:l + 1, :].broadcast_to([128, 4]), writes=[d_hp])
    S.dma("sp", hp[:, 1, :], I["gdn_a_log"][l:l + 1, :].broadcast_to([128, 4]), writes=[d_hp])
    S.act(lambda e: e.activation(out=hp[:, 2, :], in_=hp[:, 1, :], func=AF.Exp), reads=[d_hp], writes=[d_hp])
    S.dve(lambda e: e.tensor_scalar(out=hp[:, 2, :], in0=hp[:, 2, :], scalar1=-1.0, scalar2=None, op0=ALU.mult),
          reads=[d_hp], writes=[d_hp])
    nw = A.alloc((64,), F32)
    S.dma("sp", nw, I["gdn_norm_w"][l:l + 1, :].broadcast_to([128, 64]), writes=[d_hp])
    sc = A.alloc((12, NCOL), F32)
    d_sc = Dep()
    G_, GC, NGC, EGC, GLE, GLO, EGLE, EGLO, KDS, BETA, NBETA, TMP = (sc[:, i_, :] for i_ in range(12))
    g3 = G_.rearrange("p (t h) -> p t h", h=4)
    t3 = TMP.rearrange("p (t h) -> p t h", h=4)
    b3 = BETA.rearrange("p (t h) -> p t h", h=4)
    S.act(lambda e: e.activation(out=b3, in_=c.ab_sb[:, :, 4:8], func=AF.Sigmoid), reads=[c.d_ab], writes=[d_sc])
    S.dve(lambda e: e.tensor_scalar(out=NBETA, in0=BETA, scalar1=-1.0, scalar2=None, op0=ALU.mult),
          reads=[d_sc], writes=[d_sc])
    for tt in range(NT):
        S.dve(lambda e, tt=tt: e.tensor_tensor(out=t3[:, tt, :], in0=c.ab_sb[:, tt, 0:4], in1=hp[:, 0, :], op=ALU.add),
              reads=[c.d_ab, d_hp], writes=[d_sc])
    S.act(lambda e: e.activation(out=TMP, in_=TMP, func=AF.Exp), reads=[d_sc], writes=[d_sc])
    S.act(lambda e: e.activation(out=TMP, in_=TMP, func=AF.Ln, bias=1.0), reads=[d_sc], writes=[d_sc])
    for tt in range(NT):
        S.dve(lambda e, tt=tt: e.tensor_tensor(out=g3[:, tt, :], in0=t3[:, tt, :], in1=hp[:, 2, :], op=ALU.mult),
              reads=[d_sc, d_hp], writes=[d_sc])
    bk = next_bank(c)
    for i_, (m, dst) in enumerate(((BU, GC), (HE, GLE), (HO, GLO))):
        mm_group(c, bk, c.bank[bk][:, i_ * NCOL:(i_ + 1) * NCOL], [(m, G_)], reads=[d_cst, d_sc])
    for i_, dst in enumerate((GC, GLE, GLO)):
        S.dve(lambda e, i_=i_, dst=dst, bk=bk: e.tensor_copy(out=dst, in_=c.bank[bk][:, i_ * NCOL:(i_ + 1) * NCOL]),
              reads=[c.dbank[bk]], writes=[d_sc])
    S.dve(lambda e: e.tensor_scalar(out=NGC, in0=GC, scalar1=-1.0, scalar2=None, op0=ALU.mult), reads=[d_sc], writes=[d_sc])
    S.act(lambda e: e.activation(out=EGC, in_=GC, func=AF.Exp), reads=[d_sc], writes=[d_sc])
    S.act(lambda e: e.activation(out=EGLE, in_=GLE, func=AF.Exp), reads=[d_sc], writes=[d_sc])
    S.act(lambda e: e.activation(out=EGLO, in_=GLO, func=AF.Exp), reads=[d_sc], writes=[d_sc])
    S.dve(lambda e: e.tensor_tensor(out=KDS[0:64, :], in0=GLE[0:64, :], in1=GC[0:64, :], op=ALU.subtract),
          reads=[d_sc], writes=[d_sc])
    S.dve(lambda e: e.tensor_tensor(out=KDS[64:128, :], in0=GLO[64:128, :], in1=GC[64:128, :], op=ALU.subtract),
          reads=[d_sc], writes=[d_sc])
    S.act(lambda e: e.activation(out=KDS, in_=KDS, func=AF.Exp), reads=[d_sc], writes=[d_sc])

    if c.stage == "G2":
        return
    NH = 2
    DEPTH = 4
    St = A.alloc((4, 64), F32)
    Stb = A.alloc((4, 64), BF16)
    d_St = [Dep(), Dep()]
    S.dve(lambda e: e.memset(St[:], 0.0), writes=d_St)
    S.dve(lambda e: e.memset(Stb[:], 0.0), writes=d_St)

    def make_set():
        B = Ctx()
        B.tok = A.alloc((NH, 3, 128), F32)
        B.d_tok = Dep()
        B.ss = A.alloc((NH, 4), F32)
        B.d_ss = Dep()
        B.fT = A.alloc((NH, 384), BF16)
        B.tokb = A.alloc((NH, 3, 128), BF16)
        B.Pb = A.alloc((NH, 128), BF16)
        B.d_fT = Dep()
        B.FD = A.alloc((2, NH, 256), F32)
        B.d_FD = [Dep(), Dep()]
        B.MN = A.alloc((NH, 256), F32)
        B.d_MN = Dep()
        B.XY = A.alloc((2, NH, 128), F32)
        B.d_XY = [Dep(), Dep()]
        B.aqk = A.alloc((NH, 128), BF16)
        B.d_aqk = Dep()
        B.PP = [A.alloc((NH, 128), F32) for _ in range(2)]
        B.d_PP = [Dep(), Dep()]
        B.u_sb = A.alloc((NH, 64), F32)
        B.wT_sb = A.alloc((NH, 128), BF16)
        B.d_uw = Dep()
        B.vn = [A.alloc((NH, 64), BF16) for _ in range(2)]
        B.d_vn = Dep()
        B.o_sb = A.alloc((NH, 64), F32)
        B.d_o = Dep()
        B.zt = A.alloc((NH * 64,), F32)
        B.d_z = Dep()
        B.osq = A.alloc((NH, 64), F32)
        B.rst = A.alloc((8,), F32)
        B.d_rst = Dep()
        B.yb = A.alloc((NH * 64,), BF16)
        B.d_yb = Dep()
        B.qkt = B.XY[:].rearrange("p a b c -> p (a b c)")[:, 0:384].rearrange("p (a b) -> p a b", a=3)
        S.dve(lambda e: e.memset(B.fT[:], 0.0), writes=[B.d_fT])
        S.dve(lambda e: e.memset(B.tok[:], 0.0), writes=[B.d_tok])
        S.dve(lambda e: e.memset(B.tokb[:], 0.0), writes=[B.d_tok])
        S.dve(lambda e: e.memset(B.wT_sb[:], 0.0), writes=[B.d_uw])
        S.dve(lambda e: e.memset(B.vn[0][:], 0.0), writes=[B.d_vn])
        S.dve(lambda e: e.memset(B.vn[1][:], 0.0), writes=[B.d_vn])
        return B

    free_sets = [make_set() for _ in range(DEPTH)]
    h_done = set()
    HR = range(NH)

    def unit(tt, hp, B):
        tsl = slice(tt * 128, (tt + 1) * 128)
        cols = [tt * 4 + hp * 2 + h for h in HR]
        tok, ss, fT, FD, MN, XY, aqk, PP = B.tok, B.ss, B.fT, B.FD, B.MN, B.XY, B.aqk, B.PP
        d_tok, d_ss, d_fT, d_FD, d_MN, d_XY, d_aqk, d_PP = B.d_tok, B.d_ss, B.d_fT, B.d_FD, B.d_MN, B.d_XY, B.d_aqk, B.d_PP
        Fm, Dg = FD[:, 0], FD[:, 1]
        d_F, d_Dg = d_FD[0], d_FD[1]
        S.dma("sp", B.zt, c.HK[tsl, hp * 128:(hp + 1) * 128], reads=[c.dHK[tt]], writes=[B.d_z])
        for j, chn in enumerate((hp, 2 + hp, 4 + hp)):
            S.dma("sp", B.qkt[:, j, :], QCv[:, chn, tsl], reads=[c.dQC[chn]], writes=list(d_XY))
        yield
        bA = next_bank(c)
        for j in range(3):
            S.pe(lambda e, j=j: e.transpose(out=c.bank[bA][:, j * 128:(j + 1) * 128], in_=B.qkt[:, j, :], identity=idf[:]),
                 reads=list(d_XY) + [c.d_ident], writes=[c.dbank[bA]], inc=(j == 2))
        for h in HR:
            for j in range(2):
                S.act(lambda e, j=j, h=h: e.activation(out=B.osq[:, 0, :], in_=c.bank[bA][:, h * 64 + j * 128:h * 64 + j * 128 + 64],
                                                       func=AF.Square, accum_out=ss[:, h, j:j + 1]),
                      reads=[c.dbank[bA]], writes=[d_ss, B.d_rst])
        S.dve(lambda e: e.tensor_scalar(out=ss[:, :, 2:4], in0=ss[:, :, 0:2], scalar1=NORM_EPS, scalar2=None, op0=ALU.add),
              reads=[d_ss], writes=[d_ss])
        S.act(lambda e: e.activation(out=ss[:, :, 2:4], in_=ss[:, :, 2:4], func=AF.Sqrt), reads=[d_ss], writes=[d_ss])
        S.dve(lambda e: e.reciprocal(out=ss[:, :, 2:4], in_=ss[:, :, 2:4]), reads=[d_ss], writes=[d_ss])
        for h in HR:
            col = cols[h]
            qp, kp, vp = (c.bank[bA][:, h * 64 + j * 128:h * 64 + j * 128 + 64] for j in range(3))
            rd = [c.dbank[bA], d_ss, d_sc]
            S.dve(lambda e, h=h, qp=qp: e.tensor_scalar(out=tok[:, h, 0, 0:64], in0=qp, scalar1=ss[:, h, 2:3], scalar2=0.125,
                                                        op0=ALU.mult, op1=ALU.mult), reads=rd, writes=[d_tok])
            S.dve(lambda e, h=h, kp=kp: e.tensor_scalar(out=tok[:, h, 2, 0:64], in0=kp, scalar1=ss[:, h, 3:4], scalar2=None,
                                                        op0=ALU.mult), reads=rd, writes=[d_tok])
            S.dve(lambda e, h=h, vp=vp, col=col: e.tensor_scalar(out=B.tokb[:, h, 2, 0:64], in0=vp, scalar1=BETA[:, col:col + 1],
                                                                 scalar2=None, op0=ALU.mult), reads=rd, writes=[d_tok])
        yield
        for h in HR:
            col = cols[h]
            S.pool(lambda e, h=h, col=col: e.tensor_scalar(out=tok[:, h, 1, 0:64], in0=tok[:, h, 0, 0:64], scalar1=EGC[:, col:col + 1],
                                                           scalar2=1.0, op0=ALU.mult, op1=ALU.mult), reads=[d_tok, d_sc], writes=[d_tok])
            S.pool(lambda e, h=h, col=col: e.tensor_scalar(out=B.tokb[:, h, 0, 0:64], in0=tok[:, h, 2, 0:64], scalar1=BETA[:, col:col + 1],
                                                           scalar2=EGC[:, col:col + 1], op0=ALU.mult, op1=ALU.mult),
                   reads=[d_tok, d_sc], writes=[d_tok])
            S.pool(lambda e, h=h, col=col: e.tensor_scalar(out=B.tokb[:, h, 1, 0:64], in0=tok[:, h, 2, 0:64], scalar1=KDS[:, col:col + 1],
                                                           scalar2=1.0, op0=ALU.mult, op1=ALU.mult), reads=[d_tok, d_sc], writes=[d_tok])
            S.pool(lambda e, h=h, col=col: e.tensor_scalar(out=Dg[:, h, 0:128], in0=idf[:], scalar1=NGC[:, col:col + 1],
                                                           scalar2=1.0, op0=ALU.mult, op1=ALU.mult), reads=[c.d_ident, d_sc], writes=[d_Dg])
            S.pool(lambda e, h=h, col=col: e.tensor_scalar(out=Dg[:, h, 128:256], in0=idf[:], scalar1=GC[:, col:col + 1],
                                                           scalar2=1.0, op0=ALU.mult, op1=ALU.mult), reads=[c.d_ident, d_sc], writes=[d_Dg])
        yield
        for h in HR:
            bk = next_bank(c)
            for j, src in enumerate((2, 0, 1)):
                S.pe(lambda e, j=j, src=src, h=h, bk=bk: e.transpose(out=c.bank[bk][:, j * 128:(j + 1) * 128], in_=tok[:, h, src, :],
                                                                     identity=idf[:]),
                     reads=[d_tok, c.d_ident], writes=[c.dbank[bk]], inc=(j == 2))
            S.act(lambda e, h=h, bk=bk: e.activation(out=fT[:, h, :], in_=c.bank[bk][:, 0:384], func=AF.Copy),
                  reads=[c.dbank[bk]], writes=[d_fT])
        yield
        bC, bD = next_bank(c), next_bank(c)
        for h in HR:
            col = cols[h]
            o0 = h * 256
            knT = fT[:, h, 0:128]
            mm_group(c, bC, c.bank[bC][:, o0:o0 + 256], [(knT, fT[:, h, 0:256])], reads=[d_fT])
            mm_group(c, bD, c.bank[bD][:, o0:o0 + 256], [(ONES, Dg[:, h, :]), (idf[:], M12)], reads=[d_cst, d_Dg, c.d_ident])
        for h in HR:
            col = cols[h]
            o0 = h * 256
            S.act(lambda e, h=h, o0=o0, col=col: e.activation(out=Fm[:, h, 0:128], in_=c.bank[bD][:, o0:o0 + 128],
                                                              func=AF.Exp, bias=GC[:, col:col + 1]),
                  reads=[c.dbank[bD], d_sc], writes=[d_F])
            S.act(lambda e, h=h, o0=o0, col=col: e.activation(out=Fm[:, h, 128:256], in_=c.bank[bD][:, o0 + 128:o0 + 256],
                                                              func=AF.Exp, bias=NGC[:, col:col + 1]),
                  reads=[c.dbank[bD], d_sc], writes=[d_F])
            S.dve(lambda e, h=h, o0=o0, col=col: e.scalar_tensor_tensor(
                out=MN[:, h, 0:128], in0=c.bank[bC][:, o0:o0 + 128], scalar=NBETA[:, col:col + 1], in1=Fm[:, h, 0:128],
                op0=ALU.mult, op1=ALU.mult), reads=[c.dbank[bC], d_F, d_sc], writes=[d_MN])
            S.dve(lambda e, h=h, o0=o0: e.tensor_tensor(out=aqk[:, h, :], in0=c.bank[bC][:, o0 + 128:o0 + 256],
                                                        in1=Fm[:, h, 128:256], op=ALU.mult),
                  reads=[c.dbank[bC], d_F], writes=[d_aqk])
        yield
        bk = next_bank(c)
        for h in HR:
            S.pe(lambda e, h=h, bk=bk: e.transpose(out=c.bank[bk][:, h * 128:(h + 1) * 128], in_=MN[:, h, 0:128], identity=idf[:]),
                 reads=[d_MN, c.d_ident], writes=[c.dbank[bk]], inc=(h == NH - 1))
        for h in HR:
            S.act(lambda e, h=h, bk=bk: e.activation(out=MN[:, h, 128:256], in_=c.bank[bk][:, h * 128:(h + 1) * 128], func=AF.Copy),
                  reads=[c.dbank[bk]], writes=[d_MN])
        Dm, Um = PP[0], PP[1]
        W2 = NH * 128
        cc0 = FD[:, 0]
        for h in HR:
            S.pool(lambda e, h=h: e.tensor_tensor(out=cc0[:, h, :], in0=MN[:, h, :], in1=RK[:, 0, :], op=ALU.mult),
                   reads=[d_MN, d_cst], writes=[d_FD[0]])
        S.dve(lambda e: e.tensor_tensor(out=Dm[:], in0=cc0[:, :, 0:128], in1=I4[:, 0:NH, :], op=ALU.add),
              reads=[d_FD[0], d_cst], writes=[d_PP[0]])
        S.dve(lambda e: e.tensor_tensor(out=Um[:], in0=cc0[:, :, 128:256], in1=I4[:, 0:NH, :], op=ALU.add),
              reads=[d_FD[0], d_cst], writes=[d_PP[1]])
        yield
        for k in range(1, 6):
            cc, d_cc = FD[:, k % 2], d_FD[k % 2]
            for h in HR:
                S.pool(lambda e, h=h, k=k, cc=cc: e.tensor_tensor(out=cc[:, h, 0:128], in0=MN[:, h, 0:128], in1=RK[:, k, 0:128], op=ALU.mult),
                       reads=[d_MN, d_cst], writes=[d_cc])
            by = next_bank(c)
            for h in HR:
                mm_group(c, by, c.bank[by][:, h * 128:(h + 1) * 128], [(cc[:, h, 0:128], Um[:, h, :])], reads=[d_cc, d_PP[1]])
            S.act(lambda e, by=by: e.activation(out=XY[:, 1].rearrange("p a b -> p (a b)"), in_=c.bank[by][:, 0:W2], func=AF.Copy),
                  reads=[c.dbank[by]], writes=[d_XY[1]])
            yield
            bu = next_bank(c)
            for h in HR:
                mm_group(c, bu, c.bank[bu][:, h * 128:(h + 1) * 128], [(Dm[:, h, :], XY[:, 1, h, :])], reads=[d_PP[0], d_XY[1]])
            S.dve(lambda e, bu=bu: e.tensor_tensor(out=Um[:].rearrange("p a b -> p (a b)"), in0=c.bank[bu][:, 0:W2],
                                                   in1=Um[:].rearrange("p a b -> p (a b)"), op=ALU.add),
                  reads=[c.dbank[bu], d_PP[1]], writes=[d_PP[1]])
            yield
            if k < 5:
                bd = next_bank(c)
                for h in HR:
                    S.pe(lambda e, h=h, bd=bd: e.transpose(out=c.bank[bd][:, h * 128:(h + 1) * 128], in_=Um[:, h, :], identity=idf[:]),
                         reads=[d_PP[1], c.d_ident], writes=[c.dbank[bd]], inc=(h == NH - 1))
                S.act(lambda e, bd=bd: e.activation(out=Dm[:].rearrange("p a b -> p (a b)"), in_=c.bank[bd][:, 0:W2], func=AF.Copy),
                      reads=[c.dbank[bd]], writes=[d_PP[0]])
                yield
        S.act(lambda e: e.activation(out=B.Pb[:], in_=Um[:], func=AF.Copy), reads=[d_PP[1]], writes=[d_PP[1]])
        Pf, d_Pf = B.Pb, d_PP[1]
        bU, bW = next_bank(c), next_bank(c)
        for h in HR:
            mm_group(c, bU, c.bank[bU][:, h * 64:(h + 1) * 64], [(Pf[:, h, :], B.tokb[:, h, 2, 0:64])], reads=[d_Pf, d_tok])
            mm_group(c, bW, c.bank[bW][:, h * 128:(h + 1) * 128], [(B.tokb[:, h, 0, :], Pf[:, h, :])], reads=[d_Pf, d_tok])
        S.act(lambda e: e.activation(out=B.u_sb[:].rearrange("p a b -> p (a b)"), in_=c.bank[bU][:, 0:NH * 64], func=AF.Copy),
              reads=[c.dbank[bU]], writes=[B.d_uw])
        S.dve(lambda e: e.tensor_copy(out=B.wT_sb[:].rearrange("p a b -> p (a b)"), in_=c.bank[bW][:, 0:NH * 128]),
              reads=[c.dbank[bW]], writes=[B.d_uw])
        yield
        dS = d_St[hp]
        while tt > 0 and (tt - 1, hp) not in h_done:
            yield
        for pr in range(2):
            r0 = pr * 64
            rows = slice(r0, r0 + 64)
            EGL = EGLE if pr == 0 else EGLO
            vnp = B.vn[pr]
            bws = next_bank(c)
            for h in HR:
                hg = hp * 2 + h
                mm_group(c, bws, c.bank[bws][:, h * 64:(h + 1) * 64], [(B.wT_sb[:, h, :], Stb[:, hg, :])], reads=[B.d_uw, dS])
            S.dve(lambda e, rows=rows, bws=bws, vnp=vnp: e.tensor_tensor(out=vnp[rows].rearrange("p a b -> p (a b)"),
                                                                         in0=B.u_sb[rows].rearrange("p a b -> p (a b)"),
                                                                         in1=c.bank[bws][rows, 0:NH * 64], op=ALU.subtract),
                  reads=[B.d_uw, c.dbank[bws]], writes=[B.d_vn])
            yield
            bo, bs = next_bank(c), next_bank(c)
            for h in HR:
                hg = hp * 2 + h
                mm_group(c, bo, c.bank[bo][:, h * 64:(h + 1) * 64],
                         [(fT[:, h, 256:384], Stb[:, hg, :]), (aqk[:, h, :], vnp[:, h, :])],
                         reads=[d_fT, dS, d_aqk, B.d_vn])
                mm_group(c, bs, c.bank[bs][:, h * 64:(h + 1) * 64], [(B.tokb[:, h, 1, :], vnp[:, h, :])], reads=[d_tok, B.d_vn])
            S.act(lambda e, rows=rows, bo=bo: e.activation(out=B.o_sb[rows].rearrange("p a b -> p (a b)"), in_=c.bank[bo][rows, 0:NH * 64],
                                                           func=AF.Copy), reads=[c.dbank[bo]], writes=[B.d_o])
            for h in HR:
                hg = hp * 2 + h
                col = cols[h]
                S.dve(lambda e, h=h, hg=hg, col=col, bs=bs, EGL=EGL: e.scalar_tensor_tensor(
                    out=St[0:64, hg, :], in0=St[0:64, hg, :], scalar=EGL[0:64, col:col + 1], in1=c.bank[bs][0:64, h * 64:(h + 1) * 64],
                    op0=ALU.mult, op1=ALU.add), reads=[dS, c.dbank[bs], d_sc], writes=[dS])
            S.act(lambda e: e.activation(out=Stb[0:64, hp * 2:hp * 2 + 2, :], in_=St[0:64, hp * 2:hp * 2 + 2, :], func=AF.Copy),
                  reads=[dS], writes=[dS])
            yield
        h_done.add((tt, hp))
        o_sb, rst, zt, yb = B.o_sb, B.rst, B.zt, B.yb
        S.pool(lambda e: e.tensor_tensor(out=B.osq[:], in0=o_sb[:], in1=o_sb[:], op=ALU.mult), reads=[B.d_o], writes=[B.d_rst])
        S.dve(lambda e: e.tensor_reduce(out=rst[:, 0:NH], in_=B.osq[:], op=ALU.add, axis=AX.X), reads=[B.d_rst], writes=[B.d_rst])
        S.dve(lambda e: e.tensor_scalar(out=rst[:, 0:NH], in0=rst[:, 0:NH], scalar1=1.0 / 64, scalar2=NORM_EPS, op0=ALU.mult,
                                        op1=ALU.add), reads=[B.d_rst], writes=[B.d_rst])
        S.act(lambda e: e.activation(out=rst[:, 0:NH], in_=rst[:, 0:NH], func=AF.Sqrt), reads=[B.d_rst], writes=[B.d_rst])
        S.dve(lambda e: e.reciprocal(out=rst[:, 4:4 + NH], in_=rst[:, 0:NH]), reads=[B.d_rst], writes=[B.d_rst])
        S.act(lambda e: e.activation(out=zt, in_=zt, func=AF.Silu), reads=[B.d_z], writes=[B.d_z])
        for h in HR:
            S.dve(lambda e, h=h: e.scalar_tensor_tensor(out=o_sb[:, h, :], in0=o_sb[:, h, :], scalar=rst[:, 4 + h:5 + h], in1=nw,
                                                        op0=ALU.mult, op1=ALU.mult), reads=[B.d_o, B.d_rst, d_hp], writes=[B.d_o])
        S.dve(lambda e: e.tensor_tensor(out=yb, in0=o_sb[:].rearrange("p a b -> p (a b)"), in1=zt, op=ALU.mult),
              reads=[B.d_o, B.d_z], writes=[B.d_yb])
        bk = next_bank(c)
        pt = c.bank[bk][:].bitcast(BF16)
        S.pe(lambda e: e.transpose(out=pt[:, 0:128], in_=yb[:, 0:128], identity=c.ident_b[:]),
             reads=[B.d_yb, c.d_ident], writes=[c.dbank[bk]])
        S.act(lambda e: e.activation(out=XT[:, 2 + hp, tsl], in_=pt[:, 0:128], func=AF.Copy), reads=[c.dbank[bk]], writes=[c.dXT[tt]])

    units = [(tt, hp) for tt in range(NT) for hp in range(2)]
    if c.stage == "GX":
        units = units[:2]
    limit = None
    if c.stage.startswith("GY"):
        units = units[:1]
        limit = int(c.stage[2:])
    active = []
    ui = 0
    while ui < len(units) or active:
        if ui < len(units) and free_sets:
            Bs = free_sets.pop(0)
            active.append([unit(units[ui][0], units[ui][1], Bs), Bs, 0])
            ui += 1
        for item in list(active):
            try:
                if limit is not None and item[2] >= limit:
                    raise StopIteration
                item[2] += 1
                next(item[0])
            except StopIteration:
                active.remove(item)
                free_sets.append(item[1])


MNEG = -30000.0


def phase_N(c, l):
    S, I, XT = c.S, c.I, c.XT
    idf, idb = c.ident_f, c.ident_b
    S.barrier()
    A = c.arena
    A.reset()
    HTv = c.HT
    HBv = c.HB
    HKv = c.HK.rearrange("(t p) n -> p t n", p=128)
    preconvert_ffn(c, l, 2)
    QT = [A.alloc((NT, 4, 128), BF16) for _ in range(2)]
    KS = [A.alloc((SEQ,), BF16) for _ in range(2)]
    KW = [A.alloc((SEQ,), BF16) for _ in range(2)]
    VS = A.alloc((NT, 2, 65), BF16)
    VW = A.alloc((NT, 2, 65), BF16)
    KCc = [A.alloc((128,), BF16) for _ in range(2)]
    VCa = A.alloc((2, 97), F32)
    CM = A.alloc((NT, 128), BF16)
    FB = A.alloc((NT, 32), F32)
    EX = A.alloc((NT, 128), BF16)
    CAUS = A.alloc((128,), BF16)
    WINM = A.alloc((128,), BF16)
    ZB = A.alloc((512,), BF16)
    GT = A.alloc((NT, 24), F32)
    d_q, d_k, d_v, d_kc, d_vc, d_cst, d_gtt = Dep(), Dep(), Dep(), Dep(), Dep(), Dep(), Dep()
    for hk in range(2):
        S.dve(lambda e, hk=hk: e.memset(QT[hk][64:128], 0.0), writes=[d_q])
        S.dve(lambda e, hk=hk: e.memset(KS[hk][64:128], 0.0), writes=[d_k])
        S.dve(lambda e, hk=hk: e.memset(KW[hk][64:128], 0.0), writes=[d_k])
        S.dve(lambda e, hk=hk: e.memset(KCc[hk][:], 0.0), writes=[d_kc])
    for g in range(4):
        for hk in range(2):
            r0 = hk * 256 + g * 64
            S.dma("sp", QT[hk][0:64, :, g, :], HBv[r0:r0 + 64, :].rearrange("p (t k) -> p t k", k=128),
                  reads=sum((c.dHB[ch] for ch in range(4)), []), writes=[d_q])
    for hk in range(2):
        S.dma("sp", KS[hk][0:64], HBv[512 + hk * 64:576 + hk * 64, :], reads=c.dHB[4], writes=[d_k])
        S.dma("sp", KW[hk][0:64], HBv[640 + hk * 64:704 + hk * 64, :], reads=c.dHB[5], writes=[d_k])
    S.dve(lambda e: e.memset(VS[:], 1.0), writes=[d_v])
    S.dve(lambda e: e.memset(VW[:], 1.0), writes=[d_v])
    for hk in range(2):
        S.dma("pool", VS[:, :, hk, 0:64], HKv[:, :, 264 + hk * 64:328 + hk * 64], reads=list(c.dHK), writes=[d_v])
        S.dma("pool", VW[:, :, hk, 0:64], HKv[:, :, 392 + hk * 64:456 + hk * 64], reads=list(c.dHK), writes=[d_v])
    S.dma("pool", CM[0:127], I["cm"].rearrange("p (t k) -> p t k", k=128), writes=[d_cst])
    S.dma("sp", FB, I["fb"].rearrange("p (t k) -> p t k", k=32), writes=[d_cst])
    S.dve(lambda e: e.memset(EX[:], 0.0), writes=[d_cst])
    S.dma("pool", EX[0:32], I["ex"].rearrange("p (t k) -> p t k", k=128), writes=[d_cst])
    S.dma("pool", CAUS, I["caus"][:, :], writes=[d_cst])
    S.dma("pool", WINM, I["win"][:, :], writes=[d_cst])
    S.dve(lambda e: e.memset(ZB, 0.0), writes=[d_cst])
    S.act(lambda e: e.activation(out=GT[:], in_=c.gt_sb[:], func=AF.Sigmoid), reads=[c.d_gt], writes=[d_gtt])
    S.dve(lambda e: e.memset(VCa[:], 1.0), writes=[d_vc])
    for hk in range(2):
        S.dma("sp", VCa[0:127, hk, 65:97], I["ov"][:, :], writes=[d_vc])
    mark = A.off
    kT2 = A.alloc((SEQ,), F32)
    w1 = A.alloc((16, 128), F32)
    w2p = A.alloc((2, 128), F32)
    w2v = A.alloc((64,), F32)
    pe16 = A.alloc((128,), F32)
    pes = A.alloc((16,), F32)
    bias = A.alloc((2,), F32)
    hid = [A.alloc((128,), F32) for _ in range(2)]
    d_kT2, d_w1, d_w2, d_pe, d_bias = Dep(), Dep(), Dep(), Dep(), Dep()
    d_hid = [Dep(), Dep()]
    for which, chunk in (("k", 12), ("v", 13)):
        S.dma("sp", w1, I["cmp_%s_w1" % which][l].rearrange("(c p) h -> p c h", p=128), writes=[d_w1])
        if which == "k":
            S.dma("sp", w2v, I["cmp_k_w2"][l], writes=[d_w2])
        else:
            S.dma("sp", w2v, I["cmp_v_w2"][l], writes=[d_w2])
        S.dma("sp", pe16[0:16], I["cmp_pe_%s" % which][l].rearrange("(c q) d -> c (q d)", q=2), writes=[d_pe])
        bk = next_bank(c, (2, 8))
        mm_group(c, bk, c.bank[bk][:, 0:16], [(pe16[0:16, :], idf[0:16, 0:16])], reads=[d_pe, c.d_ident])
        S.dve(lambda e, bk=bk: e.tensor_copy(out=pes, in_=c.bank[bk][:, 0:16]), reads=[c.dbank[bk]], writes=[d_pe])
        bk = next_bank(c, (2, 8))
        mm_group(c, bk, c.bank[bk][:, 0:1], [(w1[:, lp, :], pes[:, lp:lp + 1]) for lp in range(16)], reads=[d_w1, d_pe])
        S.dve(lambda e, bk=bk: e.tensor_copy(out=bias[:, 0:1], in_=c.bank[bk][:, 0:1]), reads=[c.dbank[bk]], writes=[d_bias])
        for hk in range(2):
            rr = chunk * 128 + hk * 64
            S.dma("sp", kT2[0:64, :], HTv[rr:rr + 64, :], reads=c.dHT[chunk], writes=[d_kT2])
            S.dma("sp", kT2[64:128, 0:SEQ - 1], HTv[rr:rr + 64, 1:SEQ], reads=c.dHT[chunk], writes=[d_kT2])
            bk = next_bank(c, (2, 8))
            mm_group(c, bk, c.bank[bk][:, 0:127],
                     [(w1[:, lp, :], kT2[:, 2 * lp:2 * lp + 2017:16]) for lp in range(16)], reads=[d_w1, d_kT2])
            S.act(lambda e, bk=bk, hk=hk: e.activation(out=hid[hk][:, 0:127], in_=c.bank[bk][:, 0:127], func=AF.Silu,
                                                       bias=bias[:, 0:1]), reads=[c.dbank[bk], d_bias], writes=[d_hid[hk]])
        if which == "k":
            for hk in range(2):
                bk = next_bank(c, (2, 8))
                mm_group(c, bk, c.bank[bk][0:64, 0:127], [(w2v, hid[hk][:, 0:127])], reads=[d_w2, d_hid[hk]])
                S.dve(lambda e, bk=bk, hk=hk: e.tensor_copy(out=KCc[hk][0:64, 0:127], in_=c.bank[bk][0:64, 0:127]),
                      reads=[c.dbank[bk]], writes=[d_kc])
        else:
            bk = next_bank(c, (2, 8))
            for hk in range(2):
                mm_group(c, bk, c.bank[bk][0:127, hk * 64:(hk + 1) * 64], [(hid[hk][:, 0:127], w2v)], reads=[d_w2, d_hid[hk]])
            for hk in range(2):
                S.dve(lambda e, bk=bk, hk=hk: e.tensor_copy(out=VCa[0:127, hk, 0:64], in_=c.bank[bk][0:127, hk * 64:(hk + 1) * 64]),
                      reads=[c.dbank[bk]], writes=[d_vc])
    S.barrier()
    A.off = mark
    eT = [A.alloc((512,), F32) for _ in range(2)]
    d_eT = [Dep(), Dep()]
    pT = [A.alloc((512,), BF16) for _ in range(4)]
    d_pT = [Dep() for _ in range(4)]
    ycs = [A.alloc((512,), F32) for _ in range(2)]
    d_ycs = [[Dep(), Dep()], [Dep(), Dep()]]
    ycb = A.alloc((512,), BF16)
    d_ycb = Dep()
    smc = A.alloc((16,), F32)
    d_smc = Dep()
    smm = A.alloc((16,), F32)
    oTs = [A.alloc((512,), F32) for _ in range(2)]
    d_oTs = [Dep(), Dep()]
    oi = [0]
    pend_fin = []
    d_smm = Dep()
    imp = A.alloc((32,), F32)
    nsel = A.alloc((128,), BF16)
    d_imp = Dep()
    NSTs = [A.alloc((512,), BF16) for _ in range(2)]
    d_nsts = [Dep(), Dep()]
    for i_ in range(2):
        S.dve(lambda e, i_=i_: e.memset(NSTs[i_][:], 0.0), writes=[d_nsts[i_]])
    S.dve(lambda e: e.memset(nsel, 0.0), writes=[d_imp])
    IDB4 = A.alloc((512,), BF16)
    CAUS4 = A.alloc((512,), BF16)
    WINM4 = A.alloc((512,), BF16)
    for g in range(4):
        S.dve(lambda e, g=g: e.tensor_copy(out=IDB4[:, g * 128:(g + 1) * 128], in_=idb[:]), reads=[c.d_ident], writes=[d_cst])
        S.dve(lambda e, g=g: e.tensor_copy(out=CAUS4[:, g * 128:(g + 1) * 128], in_=CAUS), reads=[d_cst], writes=[d_cst])
        S.dve(lambda e, g=g: e.tensor_copy(out=WINM4[:, g * 128:(g + 1) * 128], in_=WINM), reads=[d_cst], writes=[d_cst])
    ei = [0]
    pi = [0]

    def branch_evac(tt, hk, bo, width, br, first, sm, d_sm, yc, d_yc):
        ov = c.bank[bo][:, 0:4 * width].rearrange("p (g w) -> p g w", g=4)
        S.dve(lambda e: e.tensor_scalar(out=sm[:, 0:4], in0=ov[:, :, 64], scalar1=1e-30, scalar2=None, op0=ALU.max),
              reads=[c.dbank[bo]], writes=[d_sm])
        S.dve(lambda e: e.reciprocal(out=sm[:, 0:4], in_=sm[:, 0:4]), reads=[d_sm], writes=[d_sm])
        S.dve(lambda e: e.tensor_tensor(out=sm[:, 4:8], in0=sm[:, 0:4], in1=GT[:, tt, hk * 12 + br:hk * 12 + br + 10:3], op=ALU.mult),
              reads=[d_sm, d_gtt], writes=[d_sm])
        for g in range(4):
            hq = hk * 4 + g
            if first:
                S.dve(lambda e, g=g, hq=hq: e.tensor_scalar(out=yc[:, hq * 64:(hq + 1) * 64], in0=ov[:, g, 0:64],
                                                            scalar1=sm[:, 4 + g:5 + g], scalar2=None, op0=ALU.mult),
                      reads=[c.dbank[bo], d_sm], writes=[d_yc])
            else:
                S.dve(lambda e, g=g, hq=hq: e.scalar_tensor_tensor(out=yc[:, hq * 64:(hq + 1) * 64], in0=ov[:, g, 0:64],
                                                                   scalar=sm[:, 4 + g:5 + g], in1=yc[:, hq * 64:(hq + 1) * 64],
                                                                   op0=ALU.mult, op1=ALU.add),
                      reads=[c.dbank[bo], d_sm, d_yc], writes=[d_yc])
        return ov

    def cmp_gen(tt, hk, ui):
        NST, d_nst = NSTs[ui % 2], d_nsts[ui % 2]
        yc, d_yc = ycs[tt % 2], d_ycs[tt % 2][hk]
        qv = QT[hk][:, tt, :, :].rearrange("p g k -> p (g k)")
        bs = next_bank(c, (6, 8))
        mm_group(c, bs, c.bank[bs][0:127, :], [(KCc[hk][:, 0:127], qv)], reads=[d_kc, d_q], stop=False)
        for g in range(4):
            mm_group(c, bs, c.bank[bs][0:127, g * 128:(g + 1) * 128], [(idb[0:127, 0:127], CM[0:127, tt, :])],
                     reads=[c.d_ident, d_cst], start=False, stop=(g == 3))
        e_ = eT[ei[0] % 2]
        d_e = d_eT[ei[0] % 2]
        ei[0] += 1
        S.act(lambda e: e.activation(out=e_[0:127, :], in_=c.bank[bs][0:127, :], func=AF.Exp, scale=0.125),
              reads=[c.dbank[bs]], writes=[d_e])
        yield
        bo = 2
        for g in range(4):
            mm_group(c, bo, c.bank[bo][:, g * 97:(g + 1) * 97], [(e_[0:127, g * 128:(g + 1) * 128], VCa[0:127, hk, :])],
                     reads=[d_e, d_vc])
        yield
        ov = branch_evac(tt, hk, bo, 97, 0, True, smc, d_smc, yc, d_yc)
        yield
        for g in range(4):
            if g == 0:
                S.dve(lambda e: e.tensor_scalar(out=imp, in0=ov[:, 0, 65:97], scalar1=smc[:, 0:1], scalar2=None, op0=ALU.mult),
                      reads=[c.dbank[bo], d_smc], writes=[d_imp])
            else:
                S.dve(lambda e, g=g: e.scalar_tensor_tensor(out=imp, in0=ov[:, g, 65:97], scalar=smc[:, g:g + 1], in1=imp,
                                                            op0=ALU.mult, op1=ALU.add),
                      reads=[c.dbank[bo], d_smc, d_imp], writes=[d_imp])
        yield
        S.dve(lambda e: e.tensor_tensor(out=imp, in0=imp, in1=FB[:, tt, :], op=ALU.add), reads=[d_imp, d_cst], writes=[d_imp])
        S.dve(lambda e: e.max(out=smc[:, 8:16], in_=imp), reads=[d_imp], writes=[d_smc])
        yield
        S.dve(lambda e: e.tensor_scalar(out=imp, in0=imp, scalar1=smc[:, 15:16], scalar2=-MNEG, op0=ALU.is_ge, op1=ALU.mult),
              reads=[d_imp, d_smc], writes=[d_imp])
        S.dve(lambda e: e.tensor_scalar(out=nsel[:, 0:32], in0=imp, scalar1=MNEG, scalar2=None, op0=ALU.add), reads=[d_imp], writes=[d_imp])
        yield
        bt = next_bank(c, (6, 8))
        mm_group(c, bt, c.bank[bt][:, :], [(nsel, IDB4)], reads=[d_imp, d_cst])
        S.act(lambda e: e.activation(out=NST[0:32, :], in_=c.bank[bt][0:32, :], func=AF.Copy), reads=[c.dbank[bt]], writes=[d_nst])

    def main_gen(tt, hk, ui):
        NST, d_nst = NSTs[ui % 2], d_nsts[ui % 2]
        yc, d_yc = ycs[tt % 2], d_ycs[tt % 2][hk]
        qv = QT[hk][:, tt, :, :].rearrange("p g k -> p (g k)")
        its = [(1, kb) for kb in range(0, tt + 1)] + [(2, kb) for kb in range(max(0, tt - 4), tt + 1)]
        bos = {1: 0, 2: 1}
        first_kb = {1: 0, 2: max(0, tt - 4)}
        fr = {}

        def front(i):
            br, kb = its[i]
            Kt = KS if br == 1 else KW
            bs = next_bank(c, (3, 6))
            extra = []
            if br == 1:
                extra.append((EX[:, kb, :], NST[:, :], [d_cst, d_nst]))
            if kb == tt:
                extra.append((idb[:], CAUS4, [c.d_ident, d_cst]))
            if br == 2 and kb == tt - 4:
                extra.append((idb[:], WINM4, [c.d_ident, d_cst]))
            mm_group(c, bs, c.bank[bs][:], [(Kt[hk][:, kb * 128:(kb + 1) * 128], qv)], reads=[d_k, d_q],
                     stop=(len(extra) == 0))
            for xi, (lt, rh, rds) in enumerate(extra):
                mm_group(c, bs, c.bank[bs][:], [(lt, rh)], reads=rds, start=False, stop=(xi == len(extra) - 1))
            p_ = pT[pi[0] % 4]
            d_p = d_pT[pi[0] % 4]
            pi[0] += 1
            S.act(lambda e: e.activation(out=p_, in_=c.bank[bs][:], func=AF.Exp, scale=0.125),
                  reads=[c.dbank[bs]], writes=[d_p])
            fr[i] = (p_, d_p)

        def back(i):
            br, kb = its[i]
            Vt = VS if br == 1 else VW
            p_, d_p = fr.pop(i)
            bo = bos[br]
            mm_group(c, bo, c.bank[bo][0:65, :], [(Vt[:, kb, hk, :], p_)], reads=[d_p, d_v],
                     start=(kb == first_kb[br]), stop=(kb == tt))
            for pf in pend_fin:
                pf[0] += 1
            while pend_fin and pend_fin[0][0] >= 3:
                pend_fin.pop(0)[1]()
            if kb == tt:
                while len(pend_fin) >= 2:
                    pend_fin.pop(0)[1]()
                oT_, d_oT = oTs[oi[0] % 2], d_oTs[oi[0] % 2]
                oi[0] += 1
                S.act(lambda e: e.activation(out=oT_[0:65, :], in_=c.bank[bo][0:65, :], func=AF.Copy),
                      reads=[c.dbank[bo]], writes=[d_oT])

                def fin(oT_=oT_, d_oT=d_oT, br=br):
                    bt = next_bank(c, (3, 6))
                    for g in range(4):
                        S.pe(lambda e, g=g: e.transpose(out=c.bank[bt][:, g * 65:(g + 1) * 65], in_=oT_[0:65, g * 128:(g + 1) * 128],
                                                        identity=idf[0:65, 0:65]),
                             reads=[d_oT, c.d_ident], writes=[c.dbank[bt]], inc=(g == 3))
                    branch_evac(tt, hk, bt, 65, br, False, smm, d_smm, yc, d_yc)
                pend_fin.append([0, fin])

        n = len(its)
        for i0 in range(min(3, n)):
            front(i0)
        for i in range(n):
            if i + 3 < n:
                front(i + 3)
            back(i)
            yield

    def tile_out(tt):
        while pend_fin:
            pend_fin.pop(0)[1]()
        yc = ycs[tt % 2]
        tsl = slice(tt * 128, (tt + 1) * 128)
        S.act(lambda e: e.activation(out=ycb, in_=yc, func=AF.Copy), reads=list(d_ycs[tt % 2]), writes=[d_ycb])
        bk = next_bank(c, (6, 8))
        pt = c.bank[bk][:].bitcast(BF16)
        for j in range(4):
            S.pe(lambda e, j=j: e.transpose(out=pt[:, j * 128:(j + 1) * 128], in_=ycb[:, j * 128:(j + 1) * 128], identity=idb[:]),
                 reads=[d_ycb, c.d_ident], writes=[c.dbank[bk]], inc=(j == 3))
        S.dve(lambda e: e.tensor_copy(out=XT[:, 4:8, tsl], in_=pt[:, 0:512].rearrange("p (k t) -> p k t", k=4)),
              reads=[c.dbank[bk]], writes=[c.dXT[tt]])

    units = [(tt, hk) for tt in range(NT) for hk in range(2)]
    for _ in cmp_gen(units[0][0], units[0][1], 0):
        pass
    for i, (tt, hk) in enumerate(units):
        gm = main_gen(tt, hk, i)
        gc2 = cmp_gen(units[i + 1][0], units[i + 1][1], i + 1) if i + 1 < len(units) else None
        while gm is not None or gc2 is not None:
            if gm is not None:
                try:
                    next(gm)
                except StopIteration:
                    gm = None
            if gc2 is not None:
                try:
                    next(gc2)
                except StopIteration:
                    gc2 = None
        if hk == 1:
            tile_out(tt)


def phase_O(c, l):
    S, I, X, XT = c.S, c.I, c.X, c.XT
    S.barrier()
    A = c.arena
    A.reset()
    fb = c.ffn_bufs = Ctx()
    ln_bufs(c, fb)
    wo = A.alloc((KC, D_MODEL), BF16)
    d_wo = Dep()
    S.dma("sp", fb.G, I["ln2_g"][l:l + 1, :].broadcast_to([128, D_MODEL]), writes=[fb.d_GB])
    S.dma("sp", fb.B, I["ln2_b"][l:l + 1, :].broadcast_to([128, D_MODEL]), writes=[fb.d_GB])
    S.dma("pool", wo, I["w_out"][l].rearrange("(k p) m -> p k m", p=128), writes=[d_wo])
    for tt in range(NT):
        for half in range(2):
            bk = 4 + ((tt * 2 + half) % 3)
            mm_group(c, bk, c.bank[bk][:],
                     [(XT[:, k, tt * 128:(tt + 1) * 128], wo[:, k, half * 512:(half + 1) * 512]) for k in range(KC)],
                     reads=[d_wo, c.dXT[tt]])
            residual_half(c, tt, half, bk)
        flush_transposes(c)
        layer_norm_tile(c, tt)
    flush_transposes(c)


_NC_CACHE = {}


def _consts():
    f = np.float32
    i = np.arange(128)
    same = (i[:, None] // 64) == (i[None, :] // 64)
    cst = {"c_ident": np.eye(128, dtype=f)}
    cst["c_bu"] = (same & (i[:, None] <= i[None, :])).astype(f)
    cst["c_he"] = np.broadcast_to((i[:, None] < 64), (128, 128)).astype(f)
    cst["c_ho"] = np.broadcast_to((i[:, None] >= 64), (128, 128)).astype(f)
    cst["c_m1"] = np.where(same & (i[:, None] > i[None, :]), 0.0, MNEG).astype(f)
    cst["c_m2"] = np.where(same & (i[None, :] >= i[:, None]), 0.0, MNEG).astype(f)
    n = np.arange(127)
    t = np.arange(SEQ)
    cst["c_cm"] = np.where((16 * n[:, None] + 31) <= t[None, :], 0.0, MNEG).astype(f)
    qb = t // 64
    j = np.arange(32)
    forced = (j[None, :] == 0) | (j[None, :] == qb[:, None]) | (j[None, :] == qb[:, None] - 1)
    fbv = np.where(forced, 1e3, 0.0)
    fbv = np.where(j[None, :] <= qb[:, None], fbv, NEG).astype(f)
    cst["c_fb"] = np.ascontiguousarray(fbv.reshape(NT, 128, 32).transpose(1, 0, 2).reshape(128, NT * 32))
    kb = np.arange(NT)
    exv = (j[:, None, None] == (2 * kb[None, :, None] + i[None, None, :] // 64)).astype(f)
    cst["c_ex"] = np.ascontiguousarray(exv.reshape(32, NT * 128))
    cst["c_caus"] = np.where(i[:, None] <= i[None, :], 0.0, MNEG).astype(f)
    cst["c_win"] = np.where(i[:, None] > i[None, :], 0.0, MNEG).astype(f)
    cmp_tok = (t[None, :] >= 16 * n[:, None]) & (t[None, :] < 16 * n[:, None] + 32)
    sel_tok = (t[None, :] // 64) == j[:, None]
    cst["c_ov"] = (cmp_tok.astype(f) @ sel_tok.astype(f).T / 32.0).astype(f)
    rk = np.zeros((128, 6, 256), f)
    for k in range(6):
        sz = 1 << k
        mk = ((i[:, None] // (2 * sz)) == (i[None, :] // (2 * sz))) & ((i[:, None] % (2 * sz)) >= sz) & ((i[None, :] % (2 * sz)) < sz)
        rk[:, k, 0:128] = mk
        rk[:, k, 128:256] = mk.T
    cst["c_rk"] = np.ascontiguousarray(rk.reshape(128, 6 * 256))
    return cst


def kernel(**inputs):
    if "nc" not in _NC_CACHE:
        _NC_CACHE["nc"] = build()
    nc = _NC_CACHE["nc"]
    consts = _consts()
    x = np.ascontiguousarray(inputs["x"], dtype=np.float32)
    in_maps = []
    for b in range(8):
        m = {k: np.ascontiguousarray(v) for k, v in inputs.items() if k != "x"}
        m["x"] = x[b]
        m.update(consts)
        in_maps.append(m)
    res = run_bass_kernel_spmd(nc, in_maps, core_ids=list(range(8)))
    return np.stack([r["out"] for r in res.results], axis=0).astype(np.float32)
```
